# Optimizing a Trainium2 kernel written in Bass

```python
import math
import jax, jax.numpy as jnp
from jax import lax
import numpy as np

D_MODEL = 1024
BATCH = 8
SEQ = 2048
DEPTH = 2
DEC_BATCH = 128
DEC_SEQ = 4
PAST_LEN = 16384
PAGE_SIZE = 128

N_META = 16
S5_WIDTH = 768
S5_GROUP = 16
S5_GROUPS = S5_WIDTH // S5_GROUP
S5_STATE = 64
LRU_WIDTH = 768
LRU_BLOCK = 64
LRU_BLOCKS = LRU_WIDTH // LRU_BLOCK
LRU_CONV = 4
LRU_C = 8.0
RET_HEADS = 8
RET_DK = 64
RET_DV = 128
RET_QK = RET_HEADS * RET_DK
RET_V = RET_HEADS * RET_DV
RET_CHUNK = 128
ROPE_BASE = 10000.0
N_BRANCH = 3
D_FF = 2816
FFN_CONV = 3
EPS = 1e-6
IN_SPLITS = (S5_WIDTH, LRU_WIDTH, LRU_WIDTH, RET_QK, RET_QK, RET_V, RET_V, N_BRANCH * D_MODEL)
D_IN = S5_WIDTH + 2 * LRU_WIDTH + 2 * RET_QK + 2 * RET_V + N_BRANCH * D_MODEL

kernel_name = 'hybrid_s5_rglru_retention_step'

F32 = jnp.float32


def rmsnorm(x, g):
    xf = x.astype(F32)
    y = xf * lax.rsqrt(jnp.mean(xf * xf, axis=-1, keepdims=True) + EPS)
    return (y * g.astype(F32)).astype(x.dtype)


def causal_dwconv(x, buf, w, b):
    width = w.shape[0]
    L = x.shape[1]
    xp = jnp.concatenate([buf.astype(x.dtype), x], axis=1)
    y = b + sum(xp[:, j:j + L] * w[j] for j in range(width))
    return y, xp[:, L:]


def _lin_combine(c1, c2):
    a1, b1 = c1
    a2, b2 = c2
    return a2 * a1, a2 * b1 + b2


def _cplx_combine(c1, c2):
    ar1, ai1, br1, bi1 = c1
    ar2, ai2, br2, bi2 = c2
    return (ar2 * ar1 - ai2 * ai1, ar2 * ai1 + ai2 * ar1,
            ar2 * br1 - ai2 * bi1 + br2, ar2 * bi1 + ai2 * br1 + bi2)


def s5_mixer(u, h0_re, h0_im, a_re, a_im, log_step, b_re, b_im, c_re, c_im, d_skip, w_glu, b_glu):
    Bsz, L, _ = u.shape
    uf = u.astype(F32)
    ug = uf.reshape(Bsz, L, S5_GROUPS, S5_GROUP)
    dt = jnp.exp(log_step.astype(F32))[:, None]
    lr = a_re.astype(F32)
    li = a_im.astype(F32)
    mag = jnp.exp(lr * dt)
    abar_re = mag * jnp.cos(li * dt)
    abar_im = mag * jnp.sin(li * dt)
    den = lr * lr + li * li
    nr = abar_re - 1.0
    cr = (nr * lr + abar_im * li) / den
    ci = (abar_im * lr - nr * li) / den
    bre = b_re.astype(F32)
    bim = b_im.astype(F32)
    bbar_re = cr[..., None] * bre - ci[..., None] * bim
    bbar_im = cr[..., None] * bim + ci[..., None] * bre
    bu_re = jnp.einsum('blgi,gpi->blgp', ug, bbar_re)
    bu_im = jnp.einsum('blgi,gpi->blgp', ug, bbar_im)
    h0r = h0_re.astype(F32)
    h0i = h0_im.astype(F32)
    bu_re = bu_re.at[:, 0].add(abar_re * h0r - abar_im * h0i)
    bu_im = bu_im.at[:, 0].add(abar_re * h0i + abar_im * h0r)
    ar = jnp.broadcast_to(abar_re[None, None], (1, L) + abar_re.shape)
    ai = jnp.broadcast_to(abar_im[None, None], (1, L) + abar_im.shape)
    _, _, hr, hi = lax.associative_scan(_cplx_combine, (ar, ai, bu_re, bu_im), axis=1)
    y = (jnp.einsum('blgp,gip->blgi', hr, c_re.astype(F32))
         - jnp.einsum('blgp,gip->blgi', hi, c_im.astype(F32)))
    y = y.reshape(Bsz, L, S5_WIDTH) + d_skip.astype(F32) * uf
    z = jax.nn.gelu(y)
    out = z * jax.nn.sigmoid(z @ w_glu.astype(F32) + b_glu.astype(F32))
    return out.astype(u.dtype), hr[:, -1], hi[:, -1]


def rglru_mixer(xb, gate, h0, conv_buf, conv_w, conv_b, w_a, b_a, w_x, b_x, lam):
    Bsz, L, _ = xb.shape
    xc, new_buf = causal_dwconv(xb, conv_buf, conv_w, conv_b)
    xcf = xc.astype(F32)
    xh = xcf.reshape(Bsz, L, LRU_BLOCKS, LRU_BLOCK)
    r = jax.nn.sigmoid(jnp.einsum('blhi,hij->blhj', xh, w_a.astype(F32)).reshape(Bsz, L, LRU_WIDTH) + b_a.astype(F32))
    i = jax.nn.sigmoid(jnp.einsum('blhi,hij->blhj', xh, w_x.astype(F32)).reshape(Bsz, L, LRU_WIDTH) + b_x.astype(F32))
    log_a = -LRU_C * r * jax.nn.softplus(-lam.astype(F32))
    a = jnp.exp(log_a)
    mult = jnp.sqrt(-jnp.expm1(2.0 * log_a))
    bvals = mult * (i * xcf)
    bvals = bvals.at[:, 0].add(a[:, 0] * h0.astype(F32))
    _, h = lax.associative_scan(_lin_combine, (a, bvals), axis=1)
    y = h * jax.nn.gelu(gate.astype(F32))
    return y.astype(xb.dtype), h[:, -1], new_buf


def rope(x, pos):
    half = x.shape[-1] // 2
    inv = jnp.power(ROPE_BASE, -jnp.arange(half, dtype=F32) / half)
    ang = pos.astype(F32)[:, None] * inv
    cos = jnp.cos(ang)[None, :, None, :]
    sin = jnp.sin(ang)[None, :, None, :]
    x1 = x[..., :half]
    x2 = x[..., half:]
    return jnp.concatenate([x1 * cos - x2 * sin, x2 * cos + x1 * sin], axis=-1)


def retention_chunk(s, qkv, log_gamma):
    q, k, v = qkv
    C = q.shape[2]
    n = jnp.arange(C, dtype=F32)
    diff = n[:, None] - n[None, :]
    decay = jnp.where(diff >= 0, jnp.exp(log_gamma[:, None, None] * jnp.maximum(diff, 0.0)), 0.0)
    scores = jnp.einsum('bhnd,bhmd->bhnm', q, k) * decay
    o = jnp.einsum('bhnm,bhme->bhne', scores, v)
    o = o + jnp.einsum('bhnd,bhde->bhne', q, s) * jnp.exp(log_gamma[:, None] * (n + 1.0))[..., None]
    k_dec = k * jnp.exp(log_gamma[:, None] * (C - 1.0 - n))[..., None]
    s_new = jnp.exp(log_gamma * C)[:, None, None] * s + jnp.einsum('bhmd,bhme->bhde', k_dec, v)
    return s_new, o


def retention_mixer(q, k, v, g, s0, norm_g, pos, lead, chunk):
    Bsz, L, _ = q.shape
    log_gamma = jnp.log1p(-jnp.exp2(-5.0 - jnp.arange(RET_HEADS, dtype=F32)))
    qh = rope(q.astype(F32).reshape(Bsz, L, RET_HEADS, RET_DK), pos).transpose(0, 2, 1, 3)
    kh = (rope(k.astype(F32).reshape(Bsz, L, RET_HEADS, RET_DK), pos) * RET_DK ** -0.5).transpose(0, 2, 1, 3)
    vh = v.astype(F32).reshape(Bsz, L, RET_HEADS, RET_DV).transpose(0, 2, 1, 3)
    step = lambda st, qkv: retention_chunk(st, qkv, log_gamma)
    s = s0.astype(F32)
    outs = []
    if lead > 0:
        s, o_lead = step(s, (qh[:, :, :lead], kh[:, :, :lead], vh[:, :, :lead]))
        outs.append(o_lead)
    n_chunks = (L - lead) // chunk

    def to_chunks(t):
        return t[:, :, lead:].reshape(Bsz, RET_HEADS, n_chunks, chunk, t.shape[-1]).transpose(2, 0, 1, 3, 4)

    s, o = lax.scan(step, s, (to_chunks(qh), to_chunks(kh), to_chunks(vh)))
    outs.append(o.transpose(1, 2, 0, 3, 4).reshape(Bsz, RET_HEADS, L - lead, RET_DV))
    o = jnp.concatenate(outs, axis=2).transpose(0, 2, 1, 3)
    mu = jnp.mean(o, axis=-1, keepdims=True)
    var = jnp.mean(jnp.square(o - mu), axis=-1, keepdims=True)
    o = (o - mu) * lax.rsqrt(var + EPS) * norm_g.astype(F32).reshape(RET_HEADS, RET_DV)
    y = jax.nn.silu(g.astype(F32)) * o.reshape(Bsz, L, RET_V)
    return y.astype(q.dtype), s


def layer(x, st, p, pos, lead, chunk):
    s5_re0, s5_im0, lru_h0, lru_buf0, ret_s0, ffn_buf0 = st
    Bsz, L, _ = x.shape
    h = rmsnorm(x, p['g_pre_mix'])
    z = h @ p['w_in']
    offs = np.cumsum(IN_SPLITS)[:-1].tolist()
    u_a, x_b, g_b, q, k, v, g_c, gates = jnp.split(z, offs, axis=-1)
    y_a, s5_re, s5_im = s5_mixer(u_a, s5_re0, s5_im0, p['s5_a_re'], p['s5_a_im'], p['s5_log_step'],
                                 p['s5_b_re'], p['s5_b_im'], p['s5_c_re'], p['s5_c_im'], p['s5_d'],
                                 p['s5_w_glu'], p['s5_b_glu'])
    y_b, lru_h, lru_buf = rglru_mixer(x_b, g_b, lru_h0, lru_buf0, p['lru_conv_w'], p['lru_conv_b'],
                                      p['lru_w_a'], p['lru_b_a'], p['lru_w_x'], p['lru_b_x'], p['lru_lam'])
    y_c, ret_s = retention_mixer(q, k, v, g_c, ret_s0, p['ret_norm_g'], pos, lead, chunk)
    gt = jax.nn.sigmoid(gates.reshape(Bsz, L, N_BRANCH, D_MODEL))
    m = (gt[:, :, 0] * (y_a @ p['w_branch_a'])
         + gt[:, :, 1] * (y_b @ p['w_branch_b'])
         + gt[:, :, 2] * (y_c @ p['w_branch_c']))
    x = x + rmsnorm(m @ p['w_out'], p['g_post_mix'])
    h = rmsnorm(x, p['g_pre_ffn'])
    up = h @ p['ffn_w_up']
    gf, uf = jnp.split(up, 2, axis=-1)
    gc, ffn_buf = causal_dwconv(gf, ffn_buf0, p['ffn_conv_w'], p['ffn_conv_b'])
    f = (jax.nn.gelu(gc) * uf) @ p['ffn_w_down']
    x = x + rmsnorm(f, p['g_post_ffn'])
    return x, (s5_re, s5_im, lru_h, lru_buf, ret_s, ffn_buf)


def run_group(x, states, params, pos, lead, chunk):
    new = []
    for l in range(DEPTH):
        p = {name: arr[l] for name, arr in params.items()}
        st = tuple(s[l] for s in states)
        x, st_new = layer(x, st, p, pos, lead, chunk)
        new.append(st_new)
    stacked = tuple(jnp.stack([n[i] for n in new]) for i in range(len(states)))
    return x, stacked


def setup_inputs(seed: int = 0):
    key = jax.random.key(seed)
    ks = iter(jax.random.split(key, 48))
    nrm = lambda shape, scale: scale * jax.random.normal(next(ks), shape, F32)
    x_prompt = nrm((BATCH, SEQ, D_MODEL), 1.0)
    x_sample = nrm((DEC_BATCH, DEC_SEQ, D_MODEL), 1.0)
    state_s5_re = nrm((DEPTH, DEC_BATCH, S5_GROUPS, S5_STATE), 0.5)
    state_s5_im = nrm((DEPTH, DEC_BATCH, S5_GROUPS, S5_STATE), 0.5)
    state_lru = nrm((DEPTH, DEC_BATCH, LRU_WIDTH), 0.5)
    cache_lru_conv = nrm((DEPTH, DEC_BATCH, LRU_CONV - 1, LRU_WIDTH), 1.0)
    state_ret = nrm((DEPTH, DEC_BATCH, RET_HEADS, RET_DK, RET_DV), 0.5)
    cache_ffn_conv = nrm((DEPTH, DEC_BATCH, FFN_CONV - 1, D_FF), 1.0)
    meta_tokens = nrm((N_META, D_MODEL), 1.0)
    g_pre_mix = 1.0 + nrm((DEPTH, D_MODEL), 0.02)
    g_post_mix = 1.0 + nrm((DEPTH, D_MODEL), 0.02)
    g_pre_ffn = 1.0 + nrm((DEPTH, D_MODEL), 0.02)
    g_post_ffn = 1.0 + nrm((DEPTH, D_MODEL), 0.02)
    w_in = nrm((DEPTH, D_MODEL, D_IN), D_MODEL ** -0.5)
    s5_a_re = -0.5 + nrm((DEPTH, S5_GROUPS, S5_STATE), 0.02)
    s5_a_im = jnp.pi * jnp.arange(S5_STATE, dtype=F32) + nrm((DEPTH, S5_GROUPS, S5_STATE), 0.02)
    s5_log_step = jax.random.uniform(next(ks), (DEPTH, S5_GROUPS), F32, math.log(1e-3), math.log(1e-1))
    s5_b_re = nrm((DEPTH, S5_GROUPS, S5_STATE, S5_GROUP), (2 * S5_GROUP) ** -0.5)
    s5_b_im = nrm((DEPTH, S5_GROUPS, S5_STATE, S5_GROUP), (2 * S5_GROUP) ** -0.5)
    s5_c_re = nrm((DEPTH, S5_GROUPS, S5_GROUP, S5_STATE), S5_STATE ** -0.5)
    s5_c_im = nrm((DEPTH, S5_GROUPS, S5_GROUP, S5_STATE), S5_STATE ** -0.5)
    s5_d = nrm((DEPTH, S5_WIDTH), 1.0)
    s5_w_glu = nrm((DEPTH, S5_WIDTH, S5_WIDTH), S5_WIDTH ** -0.5)
    s5_b_glu = nrm((DEPTH, S5_WIDTH), 0.01)
    lru_conv_w = nrm((DEPTH, LRU_CONV, LRU_WIDTH), LRU_CONV ** -0.5)
    lru_conv_b = nrm((DEPTH, LRU_WIDTH), 0.01)
    lru_w_a = nrm((DEPTH, LRU_BLOCKS, LRU_BLOCK, LRU_BLOCK), LRU_BLOCK ** -0.5)
    lru_b_a = nrm((DEPTH, LRU_WIDTH), 0.01)
    lru_w_x = nrm((DEPTH, LRU_BLOCKS, LRU_BLOCK, LRU_BLOCK), LRU_BLOCK ** -0.5)
    lru_b_x = nrm((DEPTH, LRU_WIDTH), 0.01)
    a_c = jax.random.uniform(next(ks), (DEPTH, LRU_WIDTH), F32, 0.9, 0.999) ** (1.0 / LRU_C)
    lru_lam = jnp.log(a_c) - jnp.log1p(-a_c)
    ret_norm_g = 1.0 + nrm((DEPTH, RET_V), 0.02)
    w_branch_a = nrm((DEPTH, S5_WIDTH, D_MODEL), S5_WIDTH ** -0.5)
    w_branch_b = nrm((DEPTH, LRU_WIDTH, D_MODEL), LRU_WIDTH ** -0.5)
    w_branch_c = nrm((DEPTH, RET_V, D_MODEL), RET_V ** -0.5)
    w_out = nrm((DEPTH, D_MODEL, D_MODEL), D_MODEL ** -0.5)
    ffn_w_up = nrm((DEPTH, D_MODEL, 2 * D_FF), D_MODEL ** -0.5)
    ffn_conv_w = nrm((DEPTH, FFN_CONV, D_FF), FFN_CONV ** -0.5)
    ffn_conv_b = nrm((DEPTH, D_FF), 0.01)
    ffn_w_down = nrm((DEPTH, D_FF, D_MODEL), D_FF ** -0.5)
    return {'x_prompt': x_prompt, 'x_sample': x_sample,
            'state_s5_re': state_s5_re, 'state_s5_im': state_s5_im, 'state_lru': state_lru,
            'cache_lru_conv': cache_lru_conv, 'state_ret': state_ret, 'cache_ffn_conv': cache_ffn_conv,
            'meta_tokens': meta_tokens,
            'g_pre_mix': g_pre_mix, 'g_post_mix': g_post_mix, 'g_pre_ffn': g_pre_ffn, 'g_post_ffn': g_post_ffn,
            'w_in': w_in, 's5_a_re': s5_a_re, 's5_a_im': s5_a_im, 's5_log_step': s5_log_step,
            's5_b_re': s5_b_re, 's5_b_im': s5_b_im, 's5_c_re': s5_c_re, 's5_c_im': s5_c_im,
            's5_d': s5_d, 's5_w_glu': s5_w_glu, 's5_b_glu': s5_b_glu,
            'lru_conv_w': lru_conv_w, 'lru_conv_b': lru_conv_b, 'lru_w_a': lru_w_a, 'lru_b_a': lru_b_a,
            'lru_w_x': lru_w_x, 'lru_b_x': lru_b_x, 'lru_lam': lru_lam, 'ret_norm_g': ret_norm_g,
            'w_branch_a': w_branch_a, 'w_branch_b': w_branch_b, 'w_branch_c': w_branch_c, 'w_out': w_out,
            'ffn_w_up': ffn_w_up, 'ffn_conv_w': ffn_conv_w, 'ffn_conv_b': ffn_conv_b, 'ffn_w_down': ffn_w_down}


def reference(x_prompt, x_sample, state_s5_re, state_s5_im, state_lru, cache_lru_conv, state_ret,
              cache_ffn_conv, meta_tokens, g_pre_mix, g_post_mix, g_pre_ffn, g_post_ffn, w_in,
              s5_a_re, s5_a_im, s5_log_step, s5_b_re, s5_b_im, s5_c_re, s5_c_im, s5_d, s5_w_glu, s5_b_glu,
              lru_conv_w, lru_conv_b, lru_w_a, lru_b_a, lru_w_x, lru_b_x, lru_lam, ret_norm_g,
              w_branch_a, w_branch_b, w_branch_c, w_out, ffn_w_up, ffn_conv_w, ffn_conv_b, ffn_w_down):
    params = {'g_pre_mix': g_pre_mix, 'g_post_mix': g_post_mix, 'g_pre_ffn': g_pre_ffn, 'g_post_ffn': g_post_ffn,
              'w_in': w_in, 's5_a_re': s5_a_re, 's5_a_im': s5_a_im, 's5_log_step': s5_log_step,
              's5_b_re': s5_b_re, 's5_b_im': s5_b_im, 's5_c_re': s5_c_re, 's5_c_im': s5_c_im,
              's5_d': s5_d, 's5_w_glu': s5_w_glu, 's5_b_glu': s5_b_glu,
              'lru_conv_w': lru_conv_w, 'lru_conv_b': lru_conv_b, 'lru_w_a': lru_w_a, 'lru_b_a': lru_b_a,
              'lru_w_x': lru_w_x, 'lru_b_x': lru_b_x, 'lru_lam': lru_lam, 'ret_norm_g': ret_norm_g,
              'w_branch_a': w_branch_a, 'w_branch_b': w_branch_b, 'w_branch_c': w_branch_c, 'w_out': w_out,
              'ffn_w_up': ffn_w_up, 'ffn_conv_w': ffn_conv_w, 'ffn_conv_b': ffn_conv_b, 'ffn_w_down': ffn_w_down}
    bp, sp, _ = x_prompt.shape
    meta = jnp.broadcast_to(meta_tokens.astype(x_prompt.dtype)[None], (bp, N_META, D_MODEL))
    xp = jnp.concatenate([meta, x_prompt], axis=1)
    zero_states = (jnp.zeros((DEPTH, bp, S5_GROUPS, S5_STATE), F32),
                   jnp.zeros((DEPTH, bp, S5_GROUPS, S5_STATE), F32),
                   jnp.zeros((DEPTH, bp, LRU_WIDTH), F32),
                   jnp.zeros((DEPTH, bp, LRU_CONV - 1, LRU_WIDTH), x_prompt.dtype),
                   jnp.zeros((DEPTH, bp, RET_HEADS, RET_DK, RET_DV), F32),
                   jnp.zeros((DEPTH, bp, FFN_CONV - 1, D_FF), x_prompt.dtype))
    pos_p = jnp.arange(N_META + sp)
    yp, (p_s5_re, p_s5_im, p_lru_h, p_lru_conv, p_ret, p_ffn_conv) = run_group(
        xp, zero_states, params, pos_p, N_META, RET_CHUNK)
    y_prompt = yp[:, N_META:]
    ds = x_sample.shape[1]
    pos_s = PAST_LEN + jnp.arange(ds)
    y_sample, (s_s5_re, s_s5_im, s_lru_h, s_lru_conv, s_ret, s_ffn_conv) = run_group(
        x_sample, (state_s5_re, state_s5_im, state_lru, cache_lru_conv, state_ret, cache_ffn_conv),
        params, pos_s, 0, ds)
    return (y_prompt, y_sample, p_s5_re, p_s5_im, p_lru_h, p_lru_conv, p_ret, p_ffn_conv,
            s_s5_re, s_s5_im, s_lru_h, s_lru_conv, s_ret, s_ffn_conv)
```

```python
import math
import numpy as np
import concourse.bass as bass
import concourse.mybir as mybir
from concourse.bass_utils import run_bass_kernel_spmd

F32 = mybir.dt.float32
BF16 = mybir.dt.bfloat16
AF = mybir.ActivationFunctionType
ALU = mybir.AluOpType
AX = mybir.AxisListType

D = 1024
DEPTH = 2
SEQ = 2048
NMETA = 16
DEC_B = 16
DEC_T = 4
PAST = 16384
DIN = 8448
DFF = 2816
EPS = 1e-6
NDS = 8


def dsize(dt):
    s = str(dt)
    if '32' in s:
        return 4
    if '16' in s:
        return 2
    if '64' in s:
        return 8
    return 1


class Sched:
    ENGS = ['pe', 'act', 'dve', 'pool', 'sp']

    def __init__(self, nc):
        self.nc = nc
        self.ops = {e: [] for e in self.ENGS}
        self.trk = {}
        self.dman = {e: 0 for e in self.ENGS}
        self.cut = None
        self.frozen = False

    def rect(self, ap):
        t = ap.tensor
        name = t.name
        pairs = ap.ap
        es = dsize(ap.dtype)
        off = int(ap.offset) * es
        sp = str(ap.space)
        if sp in ('SB', 'PSUM'):
            shp = list(t.shape)
            row = 1
            for s in shp[1:]:
                row *= s
            row *= dsize(t.dtype)
            p0 = off // row
            f0 = off % row
            pstep, pcnt = pairs[0]
            if pstep == 0:
                pcnt = 1
            ext = 0
            for st, cn in pairs[1:]:
                ext += abs(st) * (cn - 1)
            f1 = f0 + ext * es + es
            return name, p0, p0 + pcnt, f0, f1
        ext = 0
        for st, cn in pairs:
            ext += abs(st) * (cn - 1)
        return name, 0, 1, off, off + ext * es + es

    def mark(self, n):
        if self.cut is not None and n >= self.cut:
            self.frozen = True

    def add(self, eng, fn, reads, writes, dma=False):
        if self.frozen:
            return None
        idx = len(self.ops[eng])
        if dma:
            n = self.dman[eng]
            self.dman[eng] += 1
            me = ('dma', eng, n)
        else:
            me = (eng, idx)
        deps = set()
        if dma and n >= NDS:
            deps.add(('dma', eng, n - NDS))
        psum_aps = [ap for ap in list(reads) + list(writes) if str(ap.space) == 'PSUM']
        reads = [ap for ap in reads if str(ap.space) != 'PSUM']
        writes = [ap for ap in writes if str(ap.space) != 'PSUM']
        for ap in psum_aps:
            name, p0, p1, f0, f1 = self.rect(ap)
            p0 = (p0 // 32) * 32
            p1 = ((p1 + 31) // 32) * 32
            recs = self.trk.setdefault(name, [])
            keep = []
            for r in recs:
                if r[0] < p1 and p0 < r[1]:
                    if r[4] is not None:
                        deps.add(r[4])
                    if p0 <= r[0] and r[1] <= p1:
                        continue
                keep.append(r)
            keep.append([p0, p1, 0, 2048, me, {}])
            self.trk[name] = keep
        for ap in reads:
            name, p0, p1, f0, f1 = self.rect(ap)
            recs = self.trk.setdefault(name, [])
            found = None
            for r in recs:
                if r[0] < p1 and p0 < r[1] and r[2] < f1 and f0 < r[3]:
                    if r[4] is not None:
                        deps.add(r[4])
                    if r[0] == p0 and r[1] == p1 and r[2] == f0 and r[3] == f1 and r[4] is None:
                        found = r
            if found is None:
                found = [p0, p1, f0, f1, None, {}]
                recs.append(found)
            key = me if dma else eng
            found[5][key] = me
        for ap in writes:
            name, p0, p1, f0, f1 = self.rect(ap)
            recs = self.trk.setdefault(name, [])
            keep = []
            for r in recs:
                if r[0] < p1 and p0 < r[1] and r[2] < f1 and f0 < r[3]:
                    if r[4] is not None:
                        deps.add(r[4])
                    for v in r[5].values():
                        deps.add(v)
                    if p0 <= r[0] and r[1] <= p1 and f0 <= r[2] and r[3] <= f1:
                        continue
                keep.append(r)
            keep.append([p0, p1, f0, f1, me, {}])
            self.trk[name] = keep
        deps.discard(me)
        self.ops[eng].append(dict(fn=fn, deps=deps, me=me, dma=dma, sig=dma, tag=getattr(self, 'tag', '')))
        return me

    def tt(self, eng, out, in0, in1, op):
        self.add(eng, lambda e: e.tensor_tensor(out=out, in0=in0, in1=in1, op=op), [in0, in1], [out])

    def ts(self, eng, out, in0, s1, op0, s2=None, op1=None):
        rd = [in0] + [s for s in (s1, s2) if not isinstance(s, (int, float, type(None)))]
        if op1 is None:
            self.add(eng, lambda e: e.tensor_scalar(out=out, in0=in0, scalar1=s1, scalar2=None, op0=op0), rd, [out])
        else:
            self.add(eng, lambda e: e.tensor_scalar(out=out, in0=in0, scalar1=s1, scalar2=s2, op0=op0, op1=op1), rd, [out])

    def stt(self, out, in0, scalar, in1, op0, op1):
        rd = [in0, in1] + ([scalar] if not isinstance(scalar, (int, float)) else [])
        self.add('dve', lambda e: e.scalar_tensor_tensor(out=out, in0=in0, scalar=scalar, in1=in1, op0=op0, op1=op1), rd, [out])

    def act(self, out, in_, func, bias=None, scale=None):
        rd = [in_] + [s for s in (bias, scale) if not isinstance(s, (int, float, type(None)))]
        kw = {}
        if bias is not None:
            kw['bias'] = bias
        if scale is not None:
            kw['scale'] = scale
        self.add('act', lambda e: e.activation(out=out, in_=in_, func=func, **kw), rd, [out])

    def cp(self, eng, out, in_):
        if eng == 'act':
            self.add('act', lambda e: e.activation(out=out, in_=in_, func=AF.Identity), [in_], [out])
        else:
            self.add(eng, lambda e: e.tensor_copy(out=out, in_=in_), [in_], [out])

    def mm(self, out, lhsT, rhs, start, stop):
        self.add('pe', lambda e: e.matmul(out, lhsT, rhs, start=start, stop=stop), [lhsT, rhs], [out])

    def tr(self, out, in_, ident):
        self.add('pe', lambda e: e.transpose(out, in_, ident), [in_, ident], [out])

    def scan(self, out, d0, d1, init):
        rd = [d0, d1] + ([init] if not isinstance(init, (int, float)) else [])
        self.add('dve', lambda e: e.tensor_tensor_scan(out=out, data0=d0, data1=d1, initial=init, op0=ALU.mult, op1=ALU.add), rd, [out])

    def memset(self, eng, ap, val):
        self.add(eng, lambda e: e.memset(ap, val), [], [ap])

    def recip(self, out, in_):
        self.add('dve', lambda e: e.reciprocal(out=out, in_=in_), [in_], [out])

    def red(self, out, in_):
        self.add('dve', lambda e: e.tensor_reduce(out=out, in_=in_, axis=AX.X, op=ALU.add), [in_], [out])

    def dma(self, q, out, in_):
        self.add(q, lambda e: e.dma_start(out=out, in_=in_, allow_slow_non_contiguous=True), [in_], [out], dma=True)

    def emit(self, block, sems, dsems):
        ops = self.ops
        for e in self.ENGS:
            for op in ops[e]:
                for d in op['deps']:
                    if d[0] != 'dma':
                        ops[d[0]][d[1]]['sig'] = True
        cnt = {}
        for e in self.ENGS:
            c = 0
            for i, op in enumerate(ops[e]):
                if op['dma']:
                    continue
                if op['sig']:
                    c += 1
                cnt[(e, i)] = c
        finals = []
        for q in self.ENGS:
            n = self.dman[q]
            for s in range(min(NDS, n)):
                uses = (n - s + NDS - 1) // NDS
                finals.append((dsems[q][s], 16 * uses))

        def run(ename, eng):
            waited = {}
            for op in ops[ename]:
                for d in sorted(op['deps'], key=str):
                    if d[0] == 'dma':
                        sem = dsems[d[1]][d[2] % NDS]
                        val = 16 * (d[2] // NDS + 1)
                        key = ('d', d[1], d[2] % NDS)
                    else:
                        if d[0] == ename and ename == 'pe':
                            continue
                        if d[0] == ename and ename in ('dve', 'act') and (op['me'][1] - d[1]) >= 2:
                            continue
                        sem = sems[d[0]]
                        val = cnt[d]
                        key = d[0]
                    if waited.get(key, 0) >= val:
                        continue
                    waited[key] = val
                    eng.wait_ge(sem, val)
                inst = op['fn'](eng)
                if op['dma']:
                    inst.then_inc(dsems[ename][op['me'][2] % NDS], 16)
                elif op['sig']:
                    inst.then_inc(sems[ename], 1)
            if ename == 'sp':
                for sem, val in finals:
                    eng.wait_ge(sem, val)

        @block.sync
        def _(e):
            run('sp', e)

        @block.tensor
        def _(e):
            run('pe', e)

        @block.scalar
        def _(e):
            run('act', e)

        @block.vector
        def _(e):
            run('dve', e)

        @block.gpsimd
        def _(e):
            run('pool', e)


def piece_defs():
    P = []

    def add(name, src, r0, nr, c0, ncl):
        P.append((name, [(src, r0, nr, c0, ncl)]))

    for i in range(2):
        add('u%d' % i, 'w_in', 0, 1024, i * 384, 384)
    for i in range(2):
        add('glu%d' % i, 's5_w_glu', 0, 768, i * 384, 384)
    for i in range(2):
        add('wa%d' % i, 'w_branch_a', 0, 768, i * 512, 512)
    for i in range(2):
        add('ga%d' % i, 'w_in', 0, 1024, 5376 + i * 512, 512)
    for i in range(2):
        add('xb%d' % i, 'w_in', 0, 1024, 768 + i * 384, 384)
    for i in range(2):
        add('gb%d' % i, 'w_in', 0, 1024, 1536 + i * 384, 384)
    for i in range(2):
        add('wb%d' % i, 'w_branch_b', 0, 768, i * 512, 512)
    for i in range(2):
        add('gtb%d' % i, 'w_in', 0, 1024, 6400 + i * 512, 512)
    add('q', 'w_in', 0, 1024, 2304, 512)
    add('k', 'w_in', 0, 1024, 2816, 512)
    for i in range(2):
        add('v%d' % i, 'w_in', 0, 1024, 3328 + i * 512, 512)
    for i in range(2):
        add('gc%d' % i, 'w_in', 0, 1024, 4352 + i * 512, 512)
    for i in range(2):
        add('wc%d' % i, 'w_branch_c', 0, 1024, i * 512, 512)
    for i in range(2):
        add('gtc%d' % i, 'w_in', 0, 1024, 7424 + i * 512, 512)
    for i in range(2):
        add('wo%d' % i, 'w_out', 0, 1024, i * 512, 512)
    for j in range(11):
        P.append(('up%d' % j, [('ffn_w_up', 0, 1024, j * 256, 256), ('ffn_w_up', 0, 1024, DFF + j * 256, 256)]))
    for o in range(8):
        add('dn%d' % o, 'ffn_w_down', 0, 2816, o * 128, 128)
    return P


DEVP = [('B0', 3072), ('B1', 3072), ('C0', 3072), ('C1', 3072), ('T0', 4096), ('T1', 4096), ('T2', 4096), ('LG', 1536)]

WNAMES = ['g_pre_mix', 'g_post_mix', 'g_pre_ffn', 'g_post_ffn', 'w_in', 's5_a_re', 's5_a_im', 's5_log_step',
          's5_b_re', 's5_b_im', 's5_c_re', 's5_c_im', 's5_d', 's5_w_glu', 's5_b_glu', 'lru_conv_w', 'lru_conv_b',
          'lru_w_a', 'lru_b_a', 'lru_w_x', 'lru_b_x', 'lru_lam', 'ret_norm_g', 'w_branch_a', 'w_branch_b',
          'w_branch_c', 'w_out', 'ffn_w_up', 'ffn_conv_w', 'ffn_conv_b', 'ffn_w_down']


def host_consts():
    c = {}
    c['ident'] = np.eye(128, dtype=np.float32)
    pm = np.zeros((128, 128), np.float32)
    for m in range(128):
        k = m + 32 if (m % 64) < 32 else m - 32
        pm[k, m] = 1.0
    c['perm'] = pm
    half = 32
    inv = np.power(np.float32(10000.0), -np.arange(half, dtype=np.float32) / np.float32(half)).astype(np.float32)
    pos = np.concatenate([np.arange(NMETA + SEQ), np.tile(PAST + np.arange(DEC_T), DEC_B)]).astype(np.float32)
    ang = (pos[:, None] * inv[None, :]).astype(np.float32)
    cs = np.cos(ang).astype(np.float32).T
    sn = np.sin(ang).astype(np.float32).T
    rc = np.zeros((128, pos.shape[0]), np.float32)
    rs = np.zeros((128, pos.shape[0]), np.float32)
    for p in range(128):
        f = p % 32
        rc[p] = cs[f]
        rs[p] = -sn[f] if (p % 64) < 32 else sn[f]
    c['rope_c'] = rc
    c['rope_s'] = rs
    lg = np.log1p(-np.exp2(-5.0 - np.arange(8, dtype=np.float32))).astype(np.float32)
    n = np.arange(128, dtype=np.float32)
    diff = n[None, :] - n[:, None]
    dt_ = np.where(diff >= 0, np.exp(lg[:, None, None] * np.maximum(diff, 0.0)[None]), 0.0).astype(np.float32)
    c['DT'] = np.ascontiguousarray((0.125 * dt_).transpose(1, 0, 2)).astype(np.float32)
    ms = np.arange(64)
    same = (ms[:, None] // 4 == ms[None, :] // 4) & (ms[None, :] >= ms[:, None])
    dts = np.where(same[None], np.exp(lg[:, None, None] * np.maximum(ms[None, :] - ms[:, None], 0)[None].astype(np.float32)), 0.0)
    c['DTS'] = np.ascontiguousarray((0.125 * dts).transpose(1, 0, 2)).astype(np.float32)
    gq = np.zeros((128, 4, 128), np.float32)
    gqs = np.zeros((128, 4, 64), np.float32)
    gC = np.zeros((128, 3, 4, 128), np.float32)
    for p in range(128):
        for cc in range(4):
            h = 2 * cc + p // 64
            gq[p, cc] = np.exp(lg[h] * (n + 1.0))
            gqs[p, cc] = np.exp(lg[h] * ((np.arange(64) % 4) + 1.0))
            gC[p, 0, cc] = np.exp(lg[h] * 128.0)
            gC[p, 1, cc] = np.exp(lg[h] * 16.0)
    c['GQ'] = gq
    c['GQS'] = gqs
    c['GC'] = gC
    dk = np.zeros((128, 3, 512), np.float32)
    for h in range(8):
        dk[:, 0, h * 64:(h + 1) * 64] = (0.125 * np.exp(lg[h] * (127.0 - n)))[:, None]
        dk[:16, 1, h * 64:(h + 1) * 64] = (0.125 * np.exp(lg[h] * (15.0 - n[:16])))[:, None]
        dk[:64, 2, h * 64:(h + 1) * 64] = (0.125 * np.exp(lg[h] * (3.0 - (np.arange(64) % 4))))[:, None]
    c['DK'] = dk
    c['G4'] = np.exp(lg * 4.0).astype(np.float32)
    mf = np.zeros((128, 16, 64), np.float32)
    for b in range(16):
        mf[:, b, 4 * b:4 * b + 4] = 1.0
    c['MF'] = mf
    mt = np.zeros((64, 16), np.float32)
    for b in range(16):
        mt[4 * b:4 * b + 4, b] = 1.0
    c['MT'] = np.ascontiguousarray(np.pad(mt, ((0, 64), (0, 0))))
    zm = np.ones((128, 64), np.float32)
    zm[:, 0::4] = 0.0
    c['ZM'] = zm
    c['IOTA'] = np.tile(np.arange(1, 129, dtype=np.float32)[None], (128, 1))
    return c


CONST_SHAPES = None
LAST_SCHED = None


def build(debug=None):
    consts = host_consts()
    nc = bass.Bass("TRN2", target_bir_lowering=False)
    S = Sched(nc)
    if debug and 'cut' in debug:
        S.cut = int(debug.split('cut')[1])
    dr = {}

    def din(name, shape, dt=F32):
        dr[name] = nc.dram_tensor(name, list(shape), dt, kind="ExternalInput").ap()
        return dr[name]

    def dout(name, shape, dt=F32):
        dr[name] = nc.dram_tensor(name, list(shape), dt, kind="ExternalOutput").ap()
        return dr[name]

    din('x_prompt', [SEQ, D]); din('x_sample', [DEC_B * DEC_T, D])
    din('state_s5_re', [DEPTH, DEC_B, 3072]); din('state_s5_im', [DEPTH, DEC_B, 3072])
    din('state_lru', [DEPTH, DEC_B, 768]); din('cache_lru_conv', [DEPTH, DEC_B * 3, 768])
    din('state_ret', [DEPTH, DEC_B, 8, 64, 128]); din('cache_ffn_conv', [DEPTH, DEC_B * 2, DFF])
    din('meta_tokens', [NMETA, D])
    shapes = dict(g_pre_mix=[2, 1024], g_post_mix=[2, 1024], g_pre_ffn=[2, 1024], g_post_ffn=[2, 1024],
                  w_in=[2, 1024, DIN], s5_a_re=[2, 48, 64], s5_a_im=[2, 48, 64], s5_log_step=[2, 48],
                  s5_b_re=[2, 48, 64, 16], s5_b_im=[2, 48, 64, 16], s5_c_re=[2, 48, 16, 64], s5_c_im=[2, 48, 16, 64],
                  s5_d=[2, 768], s5_w_glu=[2, 768, 768], s5_b_glu=[2, 768], lru_conv_w=[2, 4, 768], lru_conv_b=[2, 768],
                  lru_w_a=[2, 12, 64, 64], lru_b_a=[2, 768], lru_w_x=[2, 12, 64, 64], lru_b_x=[2, 768], lru_lam=[2, 768],
                  ret_norm_g=[2, 1024], w_branch_a=[2, 768, 1024], w_branch_b=[2, 768, 1024], w_branch_c=[2, 1024, 1024],
                  w_out=[2, 1024, 1024], ffn_w_up=[2, 1024, 2 * DFF], ffn_conv_w=[2, 3, DFF], ffn_conv_b=[2, DFF],
                  ffn_w_down=[2, DFF, 1024])
    for n_ in WNAMES:
        din(n_, shapes[n_])
    for k_, v_ in consts.items():
        din('c_' + k_, v_.shape)
    dout('y_prompt', [SEQ, D]); dout('y_sample', [64, D])
    dout('p_s5_re', [DEPTH, 24, 128]); dout('p_s5_im', [DEPTH, 24, 128]); dout('p_lru_h', [DEPTH, 6, 128])
    dout('p_lru_conv', [DEPTH, 3, 6, 128]); dout('p_ret', [DEPTH, 8, 64, 128]); dout('p_ffn_conv', [DEPTH, 2, 22, 128])
    dout('s_s5_re', [DEPTH, DEC_B, 3072]); dout('s_s5_im', [DEPTH, DEC_B, 3072]); dout('s_lru_h', [DEPTH, DEC_B, 768])
    dout('s_lru_conv', [DEPTH, DEC_B * 3, 768]); dout('s_ret', [DEPTH, DEC_B, 8, 64, 128]); dout('s_ffn_conv', [DEPTH, DEC_B * 2, DFF])
    if debug:
        dout('dbg', [128, 8192])

    pdefs = piece_defs()
    poff = {}
    tot = 0
    for l in range(DEPTH):
        for name, srcs in pdefs:
            nr = srcs[0][2]
            E = (nr // 128) * sum(s[4] for s in srcs)
            poff[(l, name)] = (tot, E)
            tot += 128 * E
        for name, E in DEVP:
            poff[(l, name)] = (tot, E)
            tot += 128 * E
    wscr = nc.dram_tensor('wscr', [tot], BF16, kind="Internal").ap()

    def pview(l, name):
        o, E = poff[(l, name)]
        return wscr[o:o + 128 * E].rearrange("(p e) -> p e", p=128), E

    import contextlib
    es = contextlib.ExitStack()
    with es:
        def sb(name, shape, dt=F32):
            return es.enter_context(nc.sbuf_tensor(name, list(shape), dt))

        sems = {e: es.enter_context(nc.semaphore('s_' + e)) for e in ['pe', 'act', 'dve', 'pool']}
        dsems = {q: [es.enter_context(nc.semaphore('d_%s%d' % (q, i))) for i in range(NDS)] for q in Sched.ENGS}
        banks = [es.enter_context(nc.psum_tensor('ps%d' % i, [128, 512], F32)) for i in range(8)]
        bctr = [0]

        def bank():
            b = banks[bctr[0] % 8]
            bctr[0] += 1
            return b

        NRING = 5
        ring = [sb('ring%d' % i, [128, 4096], BF16) for i in range(NRING)]
        rctr = [0]
        ident = sb('ident', [128, 128]); perm = sb('perm', [128, 128]); ones = sb('ones', [128, 128])
        identb = sb('identb', [128, 128], BF16)
        negI = sb('negI', [128, 128])
        xT = sb('xT', [128, 8, 512]); hT = sb('hT', [128, 8, 512], BF16); mA = sb('mA', [128, 8, 512])
        rstd = sb('rstd', [128, 512]); sqt = [sb('sqt%d' % i, [128, 512]) for i in range(2)]
        gv = sb('gv', [128, DEPTH, 4, 8])
        ropeT = sb('ropeT', [128, 2, 512])
        cDT = sb('cDT', [128, 8, 128]); cDTS = sb('cDTS', [64, 8, 64]); cGQ = sb('cGQ', [128, 4, 128]); cGQS = sb('cGQS', [128, 4, 64])
        cGC = sb('cGC', [128, 3, 4, 128]); cDK = sb('cDK', [128, 3, 512]); cMF = sb('cMF', [128, 16, 64], BF16)
        cMT = sb('cMT', [128, 16]); cZM = sb('cZM', [128, 64])
        s5c = sb('s5c', [128, DEPTH, 8, 24]); s5c2 = sb('s5c2', [128, DEPTH, 24]); s5d = sb('s5d', [128, DEPTH, 2, 6])
        lruc = sb('lruc', [128, DEPTH, 10, 6]); ffnc = sb('ffnc', [128, DEPTH, 4, 22])
        st_s5 = sb('st_s5', [128, DEPTH, 2, 24]); st_lru = sb('st_lru', [128, DEPTH, 6]); st_lc = sb('st_lc', [128, DEPTH, 6, 3])
        st_ff = sb('st_ff', [128, DEPTH, 22, 2]); st_ret = sb('st_ret', [128, DEPTH, 4, 128]); st_retb = sb('st_retb', [128, DEPTH, 4, 128], BF16)
        yT = sb('yT', [128, 8, 512], BF16)
        gt = [sb('gt%d' % i, [128, 512]) for i in range(2)]
        fixb = sb('fixb', [128, 24, 2, 5]); stat = sb('stat', [128, 6, 8])
        WKB = 56 * 1024
        work = sb('work', [128, WKB // 2], BF16)

        class Carver:
            def __init__(self):
                self.o = 0

            def get(self, shape, dt=F32):
                Carver.mx = max(getattr(Carver, 'mx', 0), self.o)
                n = 1
                for s_ in shape[1:]:
                    n *= s_
                nb = n * dsize(dt)
                nb = (nb + 63) // 64 * 64
                assert self.o + nb <= WKB, (self.o, nb)
                v = work[0:shape[0], self.o // 2:(self.o + nb) // 2]
                if dt != BF16:
                    v = v.bitcast(dt)
                v = v[:, 0:n]
                self.o += nb
                if len(shape) == 3:
                    v = v.rearrange("p (a b) -> p a b", a=shape[1])
                elif len(shape) == 4:
                    v = v.rearrange("p (a b c) -> p a b c", a=shape[1], b=shape[2])
                return v

        cv = Carver()
        big = sb('big', [128, 3072]); sstg = cv.get([128, 3072], BF16); xtok = [cv.get([128, 1024]) for _ in range(2)]; aTs = cv.get([128, 3072])
        cv = Carver()
        uf = [cv.get([128, 512]) for _ in range(2)]; ub = [cv.get([128, 512], BF16) for _ in range(2)]
        NSL = 2
        wk = [[cv.get([128, 512]) for _ in range(6)] for s_ in range(NSL)]
        hb = [[cv.get([128, 512], BF16) for _ in range(2)] for s_ in range(NSL)]
        ytmp = [cv.get([128, 512]) for _ in range(2)]
        zT = cv.get([128, 6, 512], BF16)
        h0s = cv.get([128, 2, 24, 16]); tny = cv.get([128, 64]); st3t = [cv.get([128, 512]) for _ in range(4)]
        cv = Carver()
        xbe = cv.get([128, 6, 520]); lw = [cv.get([128, 512]) for _ in range(16)]; lwb = [cv.get([128, 512], BF16) for _ in range(4)]
        h0l = cv.get([128, 6, 16])
        cv = Carver()
        vtok = cv.get([128, 4, 1024], BF16); kdec = cv.get([128, 4, 512], BF16)
        qT = cv.get([128, 4, 512], BF16); kT = cv.get([128, 4, 512], BF16); qg = cv.get([128, 4, 512], BF16)
        gsn = cv.get([128, 1024]); otm = cv.get([128, 1024]); yct = cv.get([128, 1024], BF16)
        PT = [cv.get([128, 4, 128], BF16) for _ in range(2)]
        qgm = cv.get([128, 16, 64], BF16); vm = cv.get([128, 8, 128], BF16)
        srs = [cv.get([128, 8, 128])]; srb = cv.get([128, 8, 128], BF16)
        normg = cv.get([128, 1024]); rw = [cv.get([128, 512]) for _ in range(3)]
        cv = Carver()
        aT = cv.get([128, 22, 512], BF16); gfe = [cv.get([128, 576]) for _ in range(4)]; fw = [cv.get([128, 512]) for _ in range(8)]; h0f = cv.get([128, 22, 32])

        def ringslot():
            r = ring[rctr[0] % NRING]
            rctr[0] += 1
            return r

        def piece(l, name, dt=BF16):
            src, E = pview(l, name)
            r = ringslot()
            S.dma('sp', r[:, 0:E], src)
            if dt == F32:
                return r[:, 0:E].bitcast(F32)
            return r[:, 0:E]

        ncd = nc.allow_non_contiguous_dma(reason="small strided param loads")
        ncd.__enter__()
        S.mark(0)
        S.dma('sp', ident[:], dr['c_ident']); S.dma('sp', perm[:], dr['c_perm'])
        S.memset('dve', ones[:], 1.0)
        S.cp('dve', identb[:], ident[:])
        S.ts('dve', negI[:], ident[:], -1.0, ALU.mult)
        S.dma('sp', cDT[:], dr['c_DT']); S.dma('sp', cDTS[:], dr['c_DTS']); S.dma('sp', cGQ[:], dr['c_GQ']); S.dma('sp', cGQS[:], dr['c_GQS'])
        S.dma('sp', cGC[:], dr['c_GC']); S.dma('sp', cDK[:], dr['c_DK']); S.dma('sp', cZM[:], dr['c_ZM']); S.dma('sp', cMT[:], dr['c_MT'])
        S.dma('pool', cMF[:], dr['c_MF'])
        S.mark(10)
        for l in range(DEPTH):
            for i, nm in enumerate(['g_pre_mix', 'g_post_mix', 'g_pre_ffn', 'g_post_ffn']):
                S.dma('sp', gv[:, l, i, :], dr[nm][l].rearrange("(c p) -> p c", p=128))
            S.dma('sp', s5d[:, l, 0, :], dr['s5_d'][l].rearrange("(c p) -> p c", p=128))
            S.dma('sp', s5d[:, l, 1, :], dr['s5_b_glu'][l].rearrange("(c p) -> p c", p=128))
            for j in range(4):
                S.dma('sp', lruc[:, l, j, :], dr['lru_conv_w'][l, j].rearrange("(c p) -> p c", p=128))
            for j, nm in enumerate(['lru_conv_b', 'lru_b_a', 'lru_b_x', 'lru_lam']):
                S.dma('sp', lruc[:, l, 4 + j, :], dr[nm][l].rearrange("(c p) -> p c", p=128))
            for j in range(3):
                S.dma('sp', ffnc[:, l, j, :], dr['ffn_conv_w'][l, j].rearrange("(c p) -> p c", p=128))
            S.dma('sp', ffnc[:, l, 3, :], dr['ffn_conv_b'][l].rearrange("(c p) -> p c", p=128))
            S.mark(11)
            S.act(lruc[:, l, 8, :], lruc[:, l, 7, :], AF.Exp, scale=-1.0)
            S.act(lruc[:, l, 9, :], lruc[:, l, 8, :], AF.Ln, bias=1.0)
            S.ts('dve', lruc[:, l, 7, :], lruc[:, l, 9, :], -8.0, ALU.mult)
            S.ts('dve', lruc[:, l, 8, :], lruc[:, l, 9, :], -16.0, ALU.mult)
        S.memset('dve', st_s5[:], 0.0); S.memset('dve', st_lru[:], 0.0); S.memset('dve', st_lc[:], 0.0)
        S.memset('dve', st_ff[:], 0.0); S.memset('dve', st_ret[:], 0.0); S.memset('dve', st_retb[:], 0.0)

        S.mark(1)
        for l in range(DEPTH):
            for name, srcs in pdefs:
                dst, E = pview(l, name)
                kc = srcs[0][2] // 128
                wtot = sum(s[4] for s in srcs)
                d3 = dst.rearrange("p (k n) -> p k n", k=kc)
                c0 = 0
                for (sn_, r0, nr, cc0, ncl) in srcs:
                    src = dr[sn_][l][r0:r0 + nr, cc0:cc0 + ncl].rearrange("(k p) n -> p k n", p=128)
                    if kc > 8:
                        for k0 in range(0, kc, 8):
                            k1 = min(kc, k0 + 8)
                            S.dma('pool', d3[:, k0:k1, c0:c0 + ncl], src[:, k0:k1, :])
                    else:
                        S.dma('pool', d3[:, :, c0:c0 + ncl], src)
                    c0 += ncl

        S.mark(2)
        TWO_PI = 2.0 * math.pi

        def sincos(dst_c, dst_s, ang, tmp1, tmp2, tmp3):
            for (dst, shift) in ((dst_s, 0.0), (dst_c, 0.25)):
                S.ts('dve', tmp1, ang, 1.0 / TWO_PI, ALU.mult, shift, ALU.add)
                ti = tmp2.bitcast(mybir.dt.int32)
                S.cp('dve', ti, tmp1)
                S.cp('dve', tmp3, ti)
                S.tt('dve', tmp1, tmp1, tmp3, ALU.subtract)
                S.ts('dve', tmp3, tmp1, 0.5, ALU.is_gt)
                S.tt('dve', tmp1, tmp1, tmp3, ALU.subtract)
                S.ts('dve', tmp3, tmp1, -0.5, ALU.is_lt)
                S.tt('dve', tmp1, tmp1, tmp3, ALU.add)
                S.act(dst, tmp1, AF.Sin, scale=TWO_PI)

        for l in range(DEPTH):
            A = big
            are = A[:, 0:24]; aim = A[:, 24:48]; ls = A[:, 48:72]; dtv = A[:, 72:96]; ang = A[:, 96:120]
            t1 = A[:, 120:144]; t2 = A[:, 144:168]; t3 = A[:, 168:192]; abr = A[:, 192:216]; abi = A[:, 216:240]
            crr = A[:, 240:264]; cii = A[:, 264:288]; den = A[:, 288:312]; nr_ = A[:, 312:336]
            S.dma('sp', are, dr['s5_a_re'][l].rearrange("(k two) p -> (two p) k", two=2))
            S.dma('sp', aim, dr['s5_a_im'][l].rearrange("(k two) p -> (two p) k", two=2))
            lsv = dr['s5_log_step'][l].rearrange("(k two) -> two k", two=2)
            S.dma('sp', ls[0:64, :], lsv[0].partition_broadcast(64))
            S.dma('sp', ls[64:128, :], lsv[1].partition_broadcast(64))
            S.act(dtv, ls, AF.Exp)
            S.tt('dve', t1, are, dtv, ALU.mult)
            S.act(s5c[:, l, 0, :], t1, AF.Exp)
            S.tt('dve', ang, aim, dtv, ALU.mult)
            sincos(s5c[:, l, 1, :], s5c[:, l, 2, :], ang, t1, t2, t3)
            S.tt('dve', abr, s5c[:, l, 0, :], s5c[:, l, 1, :], ALU.mult)
            S.tt('dve', abi, s5c[:, l, 0, :], s5c[:, l, 2, :], ALU.mult)
            S.tt('dve', den, are, are, ALU.mult); S.tt('dve', t1, aim, aim, ALU.mult); S.tt('dve', den, den, t1, ALU.add)
            S.recip(den, den)
            S.ts('dve', nr_, abr, -1.0, ALU.add)
            S.tt('dve', t1, nr_, are, ALU.mult); S.tt('dve', t2, abi, aim, ALU.mult); S.tt('dve', t1, t1, t2, ALU.add)
            S.tt('dve', crr, t1, den, ALU.mult)
            S.tt('dve', t1, abi, are, ALU.mult); S.tt('dve', t2, nr_, aim, ALU.mult); S.tt('dve', t1, t1, t2, ALU.subtract)
            S.tt('dve', cii, t1, den, ALU.mult)
            S.mark(3)
            for (n_, ci_, si_) in ((128.0, s5c[:, l, 3, :], s5c[:, l, 4, :]), (16.0, s5c[:, l, 5, :], s5c[:, l, 6, :]), (4.0, s5c[:, l, 7, :], s5c2[:, l, :])):
                S.ts('dve', A[:, 336:360], ang, n_, ALU.mult)
                sincos(ci_, si_, A[:, 336:360], t1, t2, t3)
            iot = A[:, 384:512]
            S.dma('sp', iot, dr['c_IOTA'])
            for pc in range(3):
                tb = A[:, 1024:3072].rearrange("p (k r t) -> p k r t", k=8, r=2)
                angt = A[:, 512:640]; q1 = A[:, 640:768]; q2 = A[:, 768:896]; q3 = A[:, 896:1024]
                for kk in range(8):
                    k = pc * 8 + kk
                    S.ts('dve', angt, iot, ang[:, k:k + 1], ALU.mult)
                    sincos(tb[:, kk, 0, :], tb[:, kk, 1, :], angt, q1, q2, q3)
                dst, E = pview(l, 'T%d' % pc)
                S.dma('sp', dst.bitcast(F32), A[:, 1024:3072])
            S.mark(4)
            bre = A[:, 1024:1408].rearrange("p (k i) -> p k i", k=24); bim = A[:, 1408:1792].rearrange("p (k i) -> p k i", k=24)
            Bre = A[:, 1792:2176].rearrange("p (k i) -> p k i", k=24); Bim = A[:, 2176:2560].rearrange("p (k i) -> p k i", k=24)
            tq = A[:, 2560:2944].rearrange("p (k i) -> p k i", k=24)
            S.dma('sp', bre, dr['s5_b_re'][l].rearrange("(k two) p i -> (two p) k i", two=2))
            S.dma('sp', bim, dr['s5_b_im'][l].rearrange("(k two) p i -> (two p) k i", two=2))
            crb = crr.unsqueeze(2).to_broadcast([128, 24, 16]); cib = cii.unsqueeze(2).to_broadcast([128, 24, 16])
            S.tt('dve', Bre, bre, crb, ALU.mult); S.tt('dve', tq, bim, cib, ALU.mult); S.tt('dve', Bre, Bre, tq, ALU.subtract)
            S.tt('dve', Bim, bim, crb, ALU.mult); S.tt('dve', tq, bre, cib, ALU.mult); S.tt('dve', Bim, Bim, tq, ALU.add)
            stg = xtok[0][:, 0:256].rearrange("p (r c) -> p r c", r=2)
            for half in range(2):
                bo = sstg
                bo4 = bo.rearrange("p (k r s) -> p k r s", k=12, r=2)
                for kk in range(12):
                    k = half * 12 + kk
                    q = k % 4
                    S.memset('pool', stg, 0.0)
                    for r_, Bsrc in enumerate((Bre, Bim)):
                        S.cp('pool', stg[0:64, r_, 32 * q:32 * q + 16], Bsrc[0:64, k, :])
                        S.cp('pool', stg[64:128, r_, 32 * q + 16:32 * q + 32], Bsrc[64:128, k, :])
                    pb = bank()
                    for r_ in range(2):
                        S.tr(pb[:, r_ * 128:(r_ + 1) * 128], stg[:, r_, :], ident[:])
                    S.cp('act', bo4[:, kk, :, :], pb[:, 0:256].rearrange("p (r s) -> p r s", r=2))
                dst, E = pview(l, 'B%d' % half)
                S.dma('sp', dst, bo)
            S.mark(5)
            Zr = A[:, 0:3072].rearrange("p (k s) -> p k s", k=24)
            Zi = aTs.rearrange("p (k s) -> p k s", k=24)
            S.memset('pool', A[:, 0:3072], 0.0)
            S.memset('pool', aTs, 0.0)
            for Zz, nm in ((Zr, 's5_c_re'), (Zi, 's5_c_im')):
                cv = dr[nm][l].rearrange("(c q two) i p -> q two i c p", q=4, two=2)
                for q in range(4):
                    for two in range(2):
                        S.dma('sp', Zz[32 * q + 16 * two:32 * q + 16 * two + 16, q::4, two * 64:two * 64 + 64], cv[q, two])
            for half in range(2):
                cob = sstg
                co = cob.rearrange("p (k r s) -> p k r s", k=12, r=2)
                for kk in range(12):
                    k = half * 12 + kk
                    pb = bank()
                    S.tr(pb[:, 0:128], Zr[:, k, :], ident[:])
                    S.tr(pb[:, 128:256], Zi[:, k, :], ident[:])
                    S.cp('act', co[:, kk, 0, :], pb[:, 0:128])
                    S.act(co[:, kk, 1, :], pb[:, 128:256], AF.Identity, scale=-1.0)
                dst, E = pview(l, 'C%d' % half)
                S.dma('sp', dst, cob)
            S.mark(6)
            lgf = A[:, 0:1536].rearrange("p (g c s) -> p g c s", g=2, c=6)
            S.memset('pool', A[:, 0:1536], 0.0)
            for g_, nm in enumerate(('lru_w_a', 'lru_w_x')):
                wv = dr[nm][l].rearrange("(c two) i j -> two i c j", two=2)
                S.dma('sp', lgf[0:64, g_, :, 0:64], wv[0])
                S.dma('sp', lgf[64:128, g_, :, 64:128], wv[1])
            lgb = sstg[:, 0:1536]
            S.cp('dve', lgb, A[:, 0:1536])
            dst, E = pview(l, 'LG')
            S.dma('sp', dst, lgb)
            S.mark(7)

        G4 = [float(x) for x in consts['G4']]

        def W3(ap, kc):
            return ap.rearrange("p (k n) -> p k n", k=kc)

        lbank = [banks[5], banks[6], banks[7]]

        bmod = [5]

        def bank():
            b = banks[bctr[0] % bmod[0]]
            bctr[0] += 1
            return b

        def tr_in(dst_fn, src_tok, rows, nchunks):
            for c0 in range(0, nchunks, 4):
                n = min(4, nchunks - c0)
                pb = bank()
                for j in range(n):
                    S.tr(pb[:, j * rows:(j + 1) * rows], src_tok[0:rows, (c0 + j) * 128:(c0 + j + 1) * 128], ident[0:rows, 0:rows])
                S.cp('act', dst_fn(c0, n), pb[:, 0:n * rows].rearrange("p (j r) -> p j r", j=n))

        def tr_out(src_fn, dst_tok, rows, nchunks):
            for c0 in range(0, nchunks, 4):
                n = min(4, nchunks - c0)
                pb = bank()
                for j in range(n):
                    S.tr(pb[0:rows, j * 128:(j + 1) * 128], src_fn(c0 + j), ident[:])
                S.cp('act', dst_tok[0:rows, c0 * 128:(c0 + n) * 128], pb[0:rows, 0:n * 128])

        def rms_rstd(src3, nt):
            pb = bank()
            for c in range(8):
                sq = sqt[c % 2]
                S.act(sq[:, :nt], src3[:, c, :nt], AF.Square)
                S.mm(pb[:, :nt], ones[:], sq[:, :nt], start=(c == 0), stop=(c == 7))
            S.act(rstd[:, :nt], pb[:, :nt], AF.Sqrt, bias=EPS, scale=1.0 / 1024)
            S.recip(rstd[:, :nt], rstd[:, :nt])

        def norm_to_h(l, gi, nt):
            rms_rstd(xT, nt)
            for c in range(8):
                S.stt(hT[:, c, :nt], xT[:, c, :nt], gv[:, l, gi, c:c + 1], rstd[:, :nt], ALU.mult, ALU.mult)

        def resid_add(l, gi, src3, nt):
            rms_rstd(src3, nt)
            for c in range(8):
                S.stt(src3[:, c, :nt], src3[:, c, :nt], gv[:, l, gi, c:c + 1], rstd[:, :nt], ALU.mult, ALU.mult)
                S.tt('pool', xT[:, c, :nt], xT[:, c, :nt], src3[:, c, :nt], ALU.add)

        def branch(l, ysrc, kcn, wn, gn, nt, first):
            for half in range(2):
                Wp = W3(piece(l, wn + str(half)), kcn)
                Gp = W3(piece(l, gn + str(half)), 8)
                for oc4 in range(4):
                    oc = half * 4 + oc4
                    p1 = bank()
                    p2 = bank()
                    for kc in range(kcn):
                        S.mm(p1[:, :nt], Wp[:, kc, oc4 * 128:(oc4 + 1) * 128], ysrc[:, kc, :nt], start=(kc == 0), stop=(kc == kcn - 1))
                    for kc in range(8):
                        S.mm(p2[:, :nt], Gp[:, kc, oc4 * 128:(oc4 + 1) * 128], hT[:, kc, :nt], start=(kc == 0), stop=(kc == 7))
                    g = gt[oc % 2]
                    S.act(g[:, :nt], p2[:, :nt], AF.Sigmoid)
                    if first:
                        S.tt('dve', mA[:, oc, :nt], p1[:, :nt], g[:, :nt], ALU.mult)
                    else:
                        S.tt('dve', g[:, :nt], p1[:, :nt], g[:, :nt], ALU.mult)
                        S.tt('pool', mA[:, oc, :nt], mA[:, oc, :nt], g[:, :nt], ALU.add)

        def s5_phase(l, nt, kind):
            if kind == 'S':
                nseg, L = 16, 4
                cN, sN = s5c[:, l, 7, :], s5c2[:, l, :]
                for r_, nm in enumerate(('state_s5_re', 'state_s5_im')):
                    S.dma('sp', big[0:16, :], dr[nm][l])
                    for c0 in range(0, 24, 8):
                        pb = bank()
                        for j in range(8):
                            S.tr(pb[:, j * 16:(j + 1) * 16], big[0:16, (c0 + j) * 128:(c0 + j + 1) * 128], ident[0:16, 0:16])
                        S.cp('act', h0s[:, r_, c0:c0 + 8, :], pb[:, 0:128].rearrange("p (j r) -> p j r", j=8))
            else:
                L = min(nt, 128)
                nseg = nt // L
                cN, sN = (s5c[:, l, 3, :], s5c[:, l, 4, :]) if L == 128 else (s5c[:, l, 5, :], s5c[:, l, 6, :])

            def seg3(ap):
                return ap[:, :nt].rearrange("p (s t) -> p s t", s=nseg)
            stt_ = {}
            ctx = {}

            def st1(k):
                c, q = divmod(k, 4)
                if k % 12 == 0:
                    stt_['U'] = W3(piece(l, 'u%d' % (k // 12)), 8)
                    stt_['B'] = piece(l, 'B%d' % (k // 12)).rearrange("p (k r s) -> p k r s", k=12, r=2)
                if k % 8 == 0:
                    stt_['T'] = piece(l, 'T%d' % (k // 8), F32).rearrange("p (k r t) -> p k r t", k=8, r=2)
                ufc = uf[c % 2]
                ubc = ub[c % 2]
                if q == 0:
                    pu = bank()
                    for kc in range(8):
                        S.mm(pu[:, :nt], stt_['U'][:, kc, (c % 3) * 128:(c % 3 + 1) * 128], hT[:, kc, :nt], start=(kc == 0), stop=(kc == 7))
                    S.cp('act', ufc[:, :nt], pu[:, :nt])
                    S.cp('act', ubc[:, :nt], ufc[:, :nt])
                sl = k % NSL
                w = wk[sl]
                Bk = stt_['B'][:, k % 12]
                Tk = stt_['T'][:, k % 8]
                pbr = bank()
                pbi = bank()
                S.mm(pbr[:, :nt], Bk[:, 0, :], ubc[:, :nt], True, True)
                S.mm(pbi[:, :nt], Bk[:, 1, :], ubc[:, :nt], True, True)
                cb = Tk[:, 0, 0:L].unsqueeze(1).to_broadcast([128, nseg, L])
                sb_ = Tk[:, 1, 0:L].unsqueeze(1).to_broadcast([128, nseg, L])
                S.tt('dve', seg3(w[0]), seg3(pbr), cb, ALU.mult)
                S.tt('dve', seg3(w[1]), seg3(pbi), sb_, ALU.mult)
                S.tt('dve', seg3(w[2]), seg3(pbi), cb, ALU.mult)
                S.tt('dve', seg3(w[3]), seg3(pbr), sb_, ALU.mult)
                S.mm(pbr[:, :nt], ident[:], w[0][:, :nt], True, False)
                S.mm(pbr[:, :nt], ident[:], w[1][:, :nt], False, True)
                S.mm(pbi[:, :nt], ident[:], w[2][:, :nt], True, False)
                S.mm(pbi[:, :nt], negI[:], w[3][:, :nt], False, True)
                ctx[k] = (w, cb, sb_, sl, pbr, pbi)

            def st2(k):
                w, cb, sb_, sl, pwr, pwi = ctx[k]
                magk = s5c[:, l, 0, k:k + 1]
                if kind == 'P':
                    magb = magk.to_broadcast([128, L])
                    for s_ in range(nseg):
                        ire = st_s5[:, l, 0, k:k + 1] if s_ == 0 else fixb[:, k, 0, s_ - 1:s_]
                        iim = st_s5[:, l, 1, k:k + 1] if s_ == 0 else fixb[:, k, 1, s_ - 1:s_]
                        last = s_ == nseg - 1
                        dre = st_s5[:, l, 0, k:k + 1] if last else fixb[:, k, 0, s_:s_ + 1]
                        dim_ = st_s5[:, l, 1, k:k + 1] if last else fixb[:, k, 1, s_:s_ + 1]
                        gre = w[4][:, (s_ + 1) * L - 1:(s_ + 1) * L]
                        gim = w[5][:, (s_ + 1) * L - 1:(s_ + 1) * L]
                        t0 = tny[:, 4 * (k % 8):4 * (k % 8) + 1]
                        t1_ = tny[:, 4 * (k % 8) + 1:4 * (k % 8) + 2]

                        def sc_re():
                            S.scan(w[4][:, s_ * L:(s_ + 1) * L], magb, pwr[:, s_ * L:(s_ + 1) * L], ire)

                        def sc_im():
                            S.scan(w[5][:, s_ * L:(s_ + 1) * L], magb, pwi[:, s_ * L:(s_ + 1) * L], iim)

                        def f_t0():
                            S.ts('dve', t0, gim, sN[:, k:k + 1], ALU.mult)

                        def f_t1():
                            S.ts('dve', t1_, gre, sN[:, k:k + 1], ALU.mult)

                        def f_dre():
                            S.stt(dre, gre, cN[:, k:k + 1], t0, ALU.mult, ALU.subtract)

                        def f_dim():
                            S.stt(dim_, gim, cN[:, k:k + 1], t1_, ALU.mult, ALU.add)
                        if s_ % 2 == 0:
                            for f_ in (sc_re, sc_im, f_t1, f_t0, f_dim, f_dre):
                                f_()
                        else:
                            for f_ in (sc_im, sc_re, f_t0, f_t1, f_dre, f_dim):
                                f_()
                else:
                    wre3 = pwr[:, :64].rearrange("p (b t) -> p b t", t=4)
                    wim3 = pwi[:, :64].rearrange("p (b t) -> p b t", t=4)
                    S.stt(wre3[:, :, 0], h0s[:, 0, k, :], magk, wre3[:, :, 0], ALU.mult, ALU.add)
                    S.stt(wim3[:, :, 0], h0s[:, 1, k, :], magk, wim3[:, :, 0], ALU.mult, ALU.add)
                    S.ts('dve', w[1][:, :64], cZM[:, :], magk, ALU.mult)
                    S.scan(w[4][:, :64], w[1][:, :64], pwr[:, :64], 0.0)
                    S.scan(w[5][:, :64], w[1][:, :64], pwi[:, :64], 0.0)
                    g3r = w[4][:, :64].rearrange("p (b t) -> p b t", t=4)[:, :, 3]
                    g3i = w[5][:, :64].rearrange("p (b t) -> p b t", t=4)[:, :, 3]
                    S.ts('dve', tny[:, 0:16], g3i, sN[:, k:k + 1], ALU.mult)
                    S.ts('dve', tny[:, 16:32], g3r, sN[:, k:k + 1], ALU.mult)
                    S.stt(h0s[:, 0, k, :], g3r, cN[:, k:k + 1], tny[:, 0:16], ALU.mult, ALU.subtract)
                    S.stt(h0s[:, 1, k, :], g3i, cN[:, k:k + 1], tny[:, 16:32], ALU.mult, ALU.add)

            def st3(k):
                c, q = divmod(k, 4)
                w, cb, sb_, sl, pwr, pwi = ctx[k]
                if k % 12 == 0:
                    stt_['C'] = piece(l, 'C%d' % (k // 12)).rearrange("p (k r s) -> p k r s", k=12, r=2)
                py = lbank[c % 2]
                ta, tb, tc, td = st3t
                S.tt('pool', seg3(ta), seg3(w[4]), cb, ALU.mult)
                S.tt('pool', seg3(tb), seg3(w[5]), sb_, ALU.mult)
                S.tt('dve', hb[sl][0][:, :nt], ta[:, :nt], tb[:, :nt], ALU.subtract)
                S.tt('pool', seg3(tc), seg3(w[5]), cb, ALU.mult)
                S.tt('pool', seg3(td), seg3(w[4]), sb_, ALU.mult)
                S.tt('pool', hb[sl][1][:, :nt], tc[:, :nt], td[:, :nt], ALU.add)
                Ck = stt_['C'][:, k % 12]
                S.mm(py[:, :nt], Ck[:, 0, :], hb[sl][0][:, :nt], start=(q == 0), stop=False)
                S.mm(py[:, :nt], Ck[:, 1, :], hb[sl][1][:, :nt], start=False, stop=(q == 3))
                if q == 3:
                    S.stt(ytmp[c % 2][:, :nt], uf[c % 2][:, :nt], s5d[:, l, 0, c:c + 1], py[:, :nt], ALU.mult, ALU.add)
                    S.act(zT[:, c, :nt], ytmp[c % 2][:, :nt], AF.Gelu_apprx_tanh)

            for it in range(24 + 2):
                if it < 24:
                    st1(it)
                if 0 <= it - 1 < 24:
                    st2(it - 1)
                if it - 2 >= 0:
                    st3(it - 2)
            Gp = None
            for c in range(6):
                if c % 3 == 0:
                    Gp = W3(piece(l, 'glu%d' % (c // 3)), 6)
                pg = bank()
                for kc in range(6):
                    S.mm(pg[:, :nt], Gp[:, kc, (c % 3) * 128:(c % 3 + 1) * 128], zT[:, kc, :nt], start=(kc == 0), stop=(kc == 5))
                S.act(gt[c % 2][:, :nt], pg[:, :nt], AF.Sigmoid, bias=s5d[:, l, 1, c:c + 1])
                S.tt('dve', yT[:, c, :nt], zT[:, c, :nt], gt[c % 2][:, :nt], ALU.mult)
            if kind == 'S':
                for r_, nm in enumerate(('s_s5_re', 's_s5_im')):
                    tr_out(lambda cc: h0s[:, r_, cc, :], big, 16, 24)
                    S.dma('pool', dr[nm][l], big[0:16, :])
            branch(l, yT, 6, 'wa', 'ga', nt, True)

        def s5_store_prompt(l):
            for r_, nm in enumerate(('p_s5_re', 'p_s5_im')):
                pb = bank()
                S.tr(pb[0:24, 0:128], st_s5[:, l, r_, :], ident[:])
                S.cp('act', big[0:24, r_ * 128:(r_ + 1) * 128], pb[0:24, 0:128])
                S.dma('pool', dr[nm][l], big[0:24, r_ * 128:(r_ + 1) * 128])

        def lru_phase(l, nt, kind):
            if kind == 'S':
                xb4 = xbe[:, :, 0:112].rearrange("p c (b j) -> p c b j", j=7)
                S.dma('sp', big[0:48, 0:768], dr['cache_lru_conv'][l])
                for c0 in range(0, 6, 4):
                    n = min(4, 6 - c0)
                    pb = bank()
                    for j in range(n):
                        S.tr(pb[:, j * 48:(j + 1) * 48], big[0:48, (c0 + j) * 128:(c0 + j + 1) * 128], ident[0:48, 0:48])
                    for j in range(n):
                        S.cp('act', xb4[:, c0 + j, :, 0:3], pb[:, j * 48:(j + 1) * 48].rearrange("p (b j) -> p b j", j=3))
                S.dma('sp', big[0:16, 1024:1792], dr['state_lru'][l])
                pb = bank()
                for j in range(6):
                    S.tr(pb[:, j * 16:(j + 1) * 16], big[0:16, 1024 + j * 128:1024 + (j + 1) * 128], ident[0:16, 0:16])
                S.cp('act', h0l[:, :, :], pb[:, 0:96].rearrange("p (j r) -> p j r", j=6))
            XB = GB = None
            LGp = None
            for c in range(6):
                if c % 3 == 0:
                    XB = W3(piece(l, 'xb%d' % (c // 3)), 8)
                    GB = W3(piece(l, 'gb%d' % (c // 3)), 8)
                if c == 0:
                    LGp = piece(l, 'LG').rearrange("p (g c s) -> p g c s", g=2, c=6)
                px = bank()
                pg = bank()
                for kc in range(8):
                    S.mm(px[:, :nt], XB[:, kc, (c % 3) * 128:(c % 3 + 1) * 128], hT[:, kc, :nt], start=(kc == 0), stop=(kc == 7))
                for kc in range(8):
                    S.mm(pg[:, :nt], GB[:, kc, (c % 3) * 128:(c % 3 + 1) * 128], hT[:, kc, :nt], start=(kc == 0), stop=(kc == 7))
                b0, b1, b2, b3 = lw[(c % 4) * 4:(c % 4) * 4 + 4]
                xcb = lwb[c % 4]
                cwv = [lruc[:, l, j, c:c + 1] for j in range(4)]
                cbv = lruc[:, l, 4, c:c + 1]
                if kind == 'P':
                    S.cp('act', xbe[:, c, 0:3], st_lc[:, l, c, :])
                    S.cp('act', xbe[:, c, 3:3 + nt], px[:, :nt])
                    S.cp('act', st_lc[:, l, c, :], xbe[:, c, nt:nt + 3])
                    xc = b0[:, :nt]
                    S.ts('dve', xc, xbe[:, c, 3:3 + nt], cwv[3], ALU.mult, cbv, ALU.add)
                    for j in (2, 1, 0):
                        S.stt(xc, xbe[:, c, j:j + nt], cwv[j], xc, ALU.mult, ALU.add)
                else:
                    xv = xbe[:, c, 0:112].rearrange("p (b j) -> p b j", j=7)
                    S.cp('act', xv[:, :, 3:7], px[:, :64].rearrange("p (b t) -> p b t", t=4))
                    xc = b0[:, :64]
                    xc3 = xc.rearrange("p (b t) -> p b t", t=4)
                    S.ts('dve', xc3, xv[:, :, 3:7], cwv[3], ALU.mult, cbv, ALU.add)
                    for j in (2, 1, 0):
                        S.stt(xc3, xv[:, :, j:j + 4], cwv[j], xc3, ALU.mult, ALU.add)
                S.cp('act', xcb[:, :nt], xc)
                pr = bank()
                pi_ = bank()
                S.mm(pr[:, :nt], LGp[:, 0, c, :], xcb[:, :nt], True, True)
                S.mm(pi_[:, :nt], LGp[:, 1, c, :], xcb[:, :nt], True, True)
                S.act(b1[:, :nt], pr[:, :nt], AF.Sigmoid, bias=lruc[:, l, 5, c:c + 1])
                S.act(b2[:, :nt], pi_[:, :nt], AF.Sigmoid, bias=lruc[:, l, 6, c:c + 1])
                S.act(b3[:, :nt], b1[:, :nt], AF.Exp, scale=lruc[:, l, 8, c:c + 1])
                S.act(b3[:, :nt], b3[:, :nt], AF.Sqrt, bias=1.0, scale=-1.0)
                S.act(b1[:, :nt], b1[:, :nt], AF.Exp, scale=lruc[:, l, 7, c:c + 1])
                S.tt('pool', b2[:, :nt], b2[:, :nt], xc, ALU.mult)
                S.tt('pool', b2[:, :nt], b2[:, :nt], b3[:, :nt], ALU.mult)
                if kind == 'P':
                    S.scan(b0[:, :nt], b1[:, :nt], b2[:, :nt], st_lru[:, l, c:c + 1])
                    S.cp('act', st_lru[:, l, c:c + 1], b0[:, nt - 1:nt])
                else:
                    a3 = b1[:, :64].rearrange("p (b t) -> p b t", t=4)
                    bv3 = b2[:, :64].rearrange("p (b t) -> p b t", t=4)
                    S.tt('dve', b3[:, 0:16], a3[:, :, 0], h0l[:, c, :], ALU.mult)
                    S.tt('dve', bv3[:, :, 0], bv3[:, :, 0], b3[:, 0:16], ALU.add)
                    S.tt('dve', b1[:, :64], b1[:, :64], cZM[:, :], ALU.mult)
                    S.scan(b0[:, :64], b1[:, :64], b2[:, :64], 0.0)
                    S.cp('act', h0l[:, c, :], b0[:, :64].rearrange("p (b t) -> p b t", t=4)[:, :, 3])
                S.act(b3[:, :nt], pg[:, :nt], AF.Gelu_apprx_tanh)
                S.tt('dve', yT[:, c, :nt], b0[:, :nt], b3[:, :nt], ALU.mult)
            if kind == 'S':
                tr_out(lambda cc: h0l[:, cc, :], big, 16, 6)
                S.dma('pool', dr['s_lru_h'][l], big[0:16, 0:768])
                for cc in range(6):
                    xv = xbe[:, cc, 0:112].rearrange("p (b j) -> p b j", j=7)
                    S.cp('act', lw[0][:, cc * 48:(cc + 1) * 48].rearrange("p (b j) -> p b j", j=3), xv[:, :, 4:7])
                tr_out(lambda cc: lw[0][:, cc * 48:(cc + 1) * 48], big[:, 1024:1792], 48, 6)
                S.dma('pool', dr['s_lru_conv'][l], big[0:48, 1024:1792])
            branch(l, yT, 6, 'wb', 'gtb', nt, False)

        def lru_store_prompt(l):
            pb = bank()
            S.tr(pb[0:6, 0:128], st_lru[:, l, :], ident[:])
            for j in range(3):
                S.tr(pb[0:6, (j + 1) * 128:(j + 2) * 128], st_lc[:, l, :, j], ident[:])
            S.cp('act', big[0:6, 512:1024], pb[0:6, 0:512])
            S.dma('pool', dr['p_lru_h'][l], big[0:6, 512:640])
            for j in range(3):
                S.dma('pool', dr['p_lru_conv'][l, j], big[0:6, 640 + j * 128:768 + j * 128])

        def ret_phase(l, nt, kind):
            S.dma('sp', normg[:, :], dr['ret_norm_g'][l].partition_broadcast(128))
            if kind == 'S':
                L, nb, var = 64, 1, 2
            else:
                L = min(nt, 128)
                nb = nt // L
                var = 0 if L == 128 else 1
            for (nm, dstT) in (('q', qT), ('k', kT)):
                Wq = W3(piece(l, nm), 8)
                for c in range(4):
                    pq = bank()
                    for kc in range(8):
                        S.mm(pq[:, :nt], Wq[:, kc, c * 128:(c + 1) * 128], hT[:, kc, :nt], start=(kc == 0), stop=(kc == 7))
                    S.cp('act', rw[0][:, :nt], pq[:, :nt])
                    S.tt('dve', rw[1][:, :nt], pq[:, :nt], ropeT[:, 0, :nt], ALU.mult)
                    ps_ = bank()
                    S.mm(ps_[:, :nt], perm[:], rw[0][:, :nt], True, True)
                    S.tt('dve', rw[2][:, :nt], ps_[:, :nt], ropeT[:, 1, :nt], ALU.mult)
                    S.tt('pool', rw[1][:, :nt], rw[1][:, :nt], rw[2][:, :nt], ALU.add)
                    S.cp('act', dstT[:, c, :nt], rw[1][:, :nt])
                    if nm == 'q':
                        if kind == 'S':
                            S.tt('dve', qg[:, c, :64], rw[1][:, :64], cGQS[:, c, :], ALU.mult)
                        else:
                            S.tt('dve', qg[:, c, :nt].rearrange("p (s t) -> p s t", s=nb), rw[1][:, :nt].rearrange("p (s t) -> p s t", s=nb),
                                 cGQ[:, c, 0:L].unsqueeze(1).to_broadcast([128, nb, L]), ALU.mult)
            Wv = [W3(piece(l, 'v0'), 8), W3(piece(l, 'v1'), 8)]
            for b in range(nb):
                for half in range(2):
                    pv = bank()
                    for kc in range(8):
                        S.mm(pv[0:L, :], hT[:, kc, b * L:(b + 1) * L], Wv[half][:, kc, :], start=(kc == 0), stop=(kc == 7))
                    S.cp('act', vtok[0:L, b, half * 512:(half + 1) * 512], pv[0:L, :])
            Wg = [W3(piece(l, 'gc0'), 8), W3(piece(l, 'gc1'), 8)]
            for b in range(nb):
                blk = slice(b * L, (b + 1) * L)
                for half in range(2):
                    pg = bank()
                    for kc in range(8):
                        S.mm(pg[0:L, :], hT[:, kc, blk], Wg[half][:, kc, :], start=(kc == 0), stop=(kc == 7))
                    S.act(gsn[0:L, half * 512:(half + 1) * 512], pg[0:L, :], AF.Silu)
                S.tt('pool', gsn[0:L, :], gsn[0:L, :], normg[0:L, :], ALU.mult)
                for c in range(4):
                    pk = bank()
                    pkb = pk[:].bitcast(BF16)
                    S.tr(pkb[0:L, 0:128], kT[:, c, blk], identb[:])
                    S.tt('dve', kdec[0:L, b, c * 128:(c + 1) * 128], pkb[0:L, 0:128], cDK[0:L, var, c * 128:(c + 1) * 128], ALU.mult)
                for hg in range(2):
                    psE = bank()
                    psO = bank()
                    W_ = 128 if kind == 'P' else 64
                    Lc = L
                    for j in range(4):
                        h = hg * 4 + j
                        c = h // 2
                        r0 = (h % 2) * 64
                        ps = psE if h % 2 == 0 else psO
                        S.mm(ps[0:L, (j // 2) * W_:(j // 2) * W_ + Lc], kT[r0:r0 + 64, c, blk], qT[r0:r0 + 64, c, blk], True, True)
                    DTt = cDT if kind == 'P' else cDTS
                    for e_, ps in ((0, psE), (1, psO)):
                        S.tt('dve', PT[hg][0:L, e_::2, 0:Lc], ps[0:L, 0:2 * W_].rearrange("p (j n) -> p j n", j=2)[:, :, 0:Lc],
                             DTt[0:L, hg * 4 + e_:hg * 4 + 4:2, 0:Lc], ALU.mult)
                    pso = lbank[0]
                    piE = lbank[1]
                    piO = lbank[2]
                    for j in range(4):
                        h = hg * 4 + j
                        c = h // 2
                        r0 = (h % 2) * 64
                        pi_b = piE if h % 2 == 0 else piO
                        pic = slice((j // 2) * 128, (j // 2 + 1) * 128)
                        S.mm(pso[0:L, j * 128:(j + 1) * 128], PT[hg][0:L, j, 0:Lc], vtok[0:L, b, h * 128:(h + 1) * 128], True, True)
                        if kind == 'P':
                            S.mm(pi_b[0:L, pic], qg[r0:r0 + 64, c, blk], st_retb[r0:r0 + 64, l, c, :], True, True)
                        else:
                            if h % 2 == 0:
                                S.tt('dve', qgm[:, :, :], qg[:, c, 0:64].unsqueeze(1).to_broadcast([128, 16, 64]), cMF[:, :, :], ALU.mult)
                            sr = srs[0]
                            for hb_ in range(2):
                                S.dma('sp', sr[r0:r0 + 64, :, :], dr['state_ret'][l, 8 * hb_:8 * hb_ + 8, h].rearrange("b d e -> d b e"))
                                S.cp('act', srb[r0:r0 + 64, :, :], sr[r0:r0 + 64, :, :])
                                for bb in range(8):
                                    S.mm(pi_b[0:64, pic], qgm[r0:r0 + 64, 8 * hb_ + bb, :], srb[r0:r0 + 64, bb, :], (hb_ == 0 and bb == 0), (hb_ == 1 and bb == 7))
                                S.tt('dve', vm[0:64, :, :], vtok[0:64, 0, h * 128:(h + 1) * 128].unsqueeze(1).to_broadcast([64, 8, 128]),
                                     cMT[0:64, 8 * hb_:8 * hb_ + 8].unsqueeze(2).to_broadcast([64, 8, 128]), ALU.mult)
                                for qq in range(2):
                                    pu_ = bank()
                                    S.mm(pu_[r0:r0 + 64, :], kdec[0:64, 0, h * 64:(h + 1) * 64], vm[0:64, 4 * qq:4 * qq + 4, :], True, True)
                                    S.stt(sr[r0:r0 + 64, 4 * qq:4 * qq + 4, :], sr[r0:r0 + 64, 4 * qq:4 * qq + 4, :], G4[h],
                                          pu_[r0:r0 + 64, :].rearrange("p (b e) -> p b e", b=4), ALU.mult, ALU.add)
                                S.dma('pool', dr['s_ret'][l, 8 * hb_:8 * hb_ + 8, h].rearrange("b d e -> d b e"), sr[r0:r0 + 64, :, :])
                    S.cp('act', otm[0:L, hg * 512:(hg + 1) * 512], pso[0:L, :])
                    o4 = otm[0:L, hg * 512:(hg + 1) * 512].rearrange("p (j e) -> p j e", j=4)
                    S.tt('dve', o4[:, 0::2, :], o4[:, 0::2, :], piE[0:L, 0:256].rearrange("p (j e) -> p j e", j=2), ALU.add)
                    S.tt('dve', o4[:, 1::2, :], o4[:, 1::2, :], piO[0:L, 0:256].rearrange("p (j e) -> p j e", j=2), ALU.add)
                    S.act(rw[0][0:L, :], otm[0:L, hg * 512:(hg + 1) * 512], AF.Square)
                    S.red(stat[0:L, 0, hg * 4:hg * 4 + 4], otm[0:L, hg * 512:(hg + 1) * 512].rearrange("p (j e) -> p j e", j=4))
                    S.red(stat[0:L, 1, hg * 4:hg * 4 + 4], rw[0][0:L, :].rearrange("p (j e) -> p j e", j=4))
                S.ts('dve', stat[0:L, 2, :], stat[0:L, 0, :], 1.0 / 128, ALU.mult)
                S.tt('dve', stat[0:L, 3, :], stat[0:L, 2, :], stat[0:L, 2, :], ALU.mult)
                S.stt(stat[0:L, 4, :], stat[0:L, 1, :], 1.0 / 128, stat[0:L, 3, :], ALU.mult, ALU.subtract)
                S.act(stat[0:L, 5, :], stat[0:L, 4, :], AF.Sqrt, bias=EPS)
                S.recip(stat[0:L, 5, :], stat[0:L, 5, :])
                o3 = otm[0:L, :].rearrange("p (h e) -> p h e", h=8)
                S.tt('dve', o3, o3, stat[0:L, 2, :].unsqueeze(2).to_broadcast([L, 8, 128]), ALU.subtract)
                S.tt('dve', o3, o3, stat[0:L, 5, :].unsqueeze(2).to_broadcast([L, 8, 128]), ALU.mult)
                S.tt('dve', yct[0:L, :], otm[0:L, :], gsn[0:L, :], ALU.mult)
                pt = bank()
                ptb = pt[:].bitcast(BF16)
                for h in range(8):
                    S.tr(ptb[:, h * L:(h + 1) * L], yct[0:L, h * 128:(h + 1) * 128], identb[0:L, 0:L])
                S.cp('act', yT[:, :, blk], ptb[:, 0:8 * L].rearrange("p (h n) -> p h n", h=8))
                if kind == 'P':
                    pu_ = bank()
                    for h in range(8):
                        c = h // 2
                        r0 = (h % 2) * 64
                        S.mm(pu_[r0:r0 + 64, c * 128:(c + 1) * 128], kdec[0:L, b, h * 64:(h + 1) * 64], vtok[0:L, b, h * 128:(h + 1) * 128], True, True)
                    S.tt('pool', st_ret[:, l, :, :], st_ret[:, l, :, :], cGC[:, var, :, :], ALU.mult)
                    S.tt('dve', st_ret[:, l, :, :], st_ret[:, l, :, :], pu_[:, :].rearrange("p (c e) -> p c e", c=4), ALU.add)
                    S.cp('act', st_retb[:, l, :, :], st_ret[:, l, :, :])
            branch(l, yT, 8, 'wc', 'gtc', nt, False)

        def ret_store_prompt(l):
            S.dma('pool', dr['p_ret'][l].rearrange("(c two) d e -> (two d) c e", two=2), st_ret[:, l, :, :])

        def outproj(l, nt):
            for c in range(8):
                S.cp('act', hT[:, c, :nt], mA[:, c, :nt])
            for half in range(2):
                Wo = W3(piece(l, 'wo%d' % half), 8)
                for oc4 in range(4):
                    oc = half * 4 + oc4
                    po = bank()
                    for kc in range(8):
                        S.mm(po[:, :nt], Wo[:, kc, oc4 * 128:(oc4 + 1) * 128], hT[:, kc, :nt], start=(kc == 0), stop=(kc == 7))
                    S.cp('act', mA[:, oc, :nt], po[:, :nt])
            resid_add(l, 1, mA, nt)

        def ffn_phase(l, nt, kind):
            norm_to_h(l, 2, nt)
            if kind == 'S':
                S.dma('sp', big[0:32, 0:DFF], dr['cache_ffn_conv'][l])
                for c0 in range(0, 22, 4):
                    n = min(4, 22 - c0)
                    pb = bank()
                    for j in range(n):
                        S.tr(pb[:, j * 32:(j + 1) * 32], big[0:32, (c0 + j) * 128:(c0 + j + 1) * 128], ident[0:32, 0:32])
                    S.cp('act', h0f[:, c0:c0 + n, :], pb[:, 0:n * 32].rearrange("p (j r) -> p j r", j=n))
            for jj in range(11):
                Up = W3(piece(l, 'up%d' % jj), 8)
                for s_ in range(2):
                    ch = 2 * jj + s_
                    pgf = bank()
                    puf = bank()
                    for kc in range(8):
                        S.mm(pgf[:, :nt], Up[:, kc, s_ * 128:(s_ + 1) * 128], hT[:, kc, :nt], start=(kc == 0), stop=(kc == 7))
                    for kc in range(8):
                        S.mm(puf[:, :nt], Up[:, kc, 256 + s_ * 128:256 + (s_ + 1) * 128], hT[:, kc, :nt], start=(kc == 0), stop=(kc == 7))
                    ge = gfe[ch % 4]
                    acc = fw[ch % 4]
                    w0, w1, w2, wb_ = [ffnc[:, l, i, ch:ch + 1] for i in range(4)]
                    if kind == 'P':
                        S.cp('act', ge[:, 0:2], st_ff[:, l, ch, :])
                        S.cp('act', ge[:, 2:2 + nt], pgf[:, :nt])
                        S.cp('act', st_ff[:, l, ch, :], ge[:, nt:nt + 2])
                        a_ = acc[:, :nt]
                        S.ts('dve', a_, ge[:, 2:2 + nt], w2, ALU.mult, wb_, ALU.add)
                        S.stt(a_, ge[:, 1:1 + nt], w1, a_, ALU.mult, ALU.add)
                        S.stt(a_, ge[:, 0:nt], w0, a_, ALU.mult, ALU.add)
                    else:
                        g3 = ge[:, 0:96].rearrange("p (b j) -> p b j", j=6)
                        S.cp('act', g3[:, :, 0:2], h0f[:, ch, :].rearrange("p (b j) -> p b j", j=2))
                        S.cp('act', g3[:, :, 2:6], pgf[:, :64].rearrange("p (b t) -> p b t", t=4))
                        S.cp('act', h0f[:, ch, :].rearrange("p (b j) -> p b j", j=2), g3[:, :, 4:6])
                        a_ = acc[:, :64]
                        a3_ = a_.rearrange("p (b t) -> p b t", t=4)
                        S.ts('dve', a3_, g3[:, :, 2:6], w2, ALU.mult, wb_, ALU.add)
                        S.stt(a3_, g3[:, :, 1:5], w1, a3_, ALU.mult, ALU.add)
                        S.stt(a3_, g3[:, :, 0:4], w0, a3_, ALU.mult, ALU.add)
                    gl = fw[4 + ch % 4]
                    S.act(gl[:, :nt], a_, AF.Gelu_apprx_tanh)
                    S.tt('dve', aT[:, ch, :nt], gl[:, :nt], puf[:, :nt], ALU.mult)
            for oc in range(8):
                Dn = W3(piece(l, 'dn%d' % oc), 22)
                pf = bank()
                for kc in range(22):
                    S.mm(pf[:, :nt], Dn[:, kc, :], aT[:, kc, :nt], start=(kc == 0), stop=(kc == 21))
                S.cp('act', mA[:, oc, :nt], pf[:, :nt])
            if kind == 'S':
                tr_out(lambda cc: h0f[:, cc, :], big, 32, 22)
                S.dma('pool', dr['s_ffn_conv'][l], big[0:32, 0:DFF])
            resid_add(l, 3, mA, nt)

        def ffn_store_prompt(l):
            pb = bank()
            for j in range(2):
                S.tr(pb[0:22, j * 128:(j + 1) * 128], st_ff[:, l, :, j], ident[:])
            S.cp('act', big[0:22, 1024:1280], pb[0:22, 0:256])
            for j in range(2):
                S.dma('pool', dr['p_ffn_conv'][l, j], big[0:22, 1024 + j * 128:1152 + j * 128])

        S.tag = 'main_pre'
        tiles = [('P', 512, 0), ('P', 512, 512), ('P', 512, 1024), ('P', 512, 1536), ('P', 16, 2048), ('S', 64, 2064)]
        if debug and debug.startswith('T'):
            tiles = [tiles[int(ch_)] for ch_ in debug[1:].split('cut')[0]]
        nlayers = DEPTH
        xi = [0]
        for (kind, nt, pos0) in tiles:
            if kind == 'P':
                L = min(nt, 128)
                for b in range(nt // L):
                    xt_ = xtok[xi[0] % 2]
                    xi[0] += 1
                    t0 = pos0 + b * L
                    if t0 == 0:
                        S.dma('sp', xt_[0:16, :], dr['meta_tokens'])
                        S.dma('sp', xt_[16:128, :], dr['x_prompt'][0:112, :])
                    else:
                        S.dma('sp', xt_[0:L, :], dr['x_prompt'][t0 - 16:t0 - 16 + L, :])
                    tr_in(lambda c0, n, b=b, L=L: xT[:, c0:c0 + n, b * L:(b + 1) * L], xt_, L, 8)
            else:
                xt_ = xtok[xi[0] % 2]
                xi[0] += 1
                S.dma('sp', xt_[0:64, :], dr['x_sample'])
                tr_in(lambda c0, n: xT[:, c0:c0 + n, 0:64], xt_, 64, 8)
            S.dma('sp', ropeT[:, 0, :nt], dr['c_rope_c'][:, pos0:pos0 + nt])
            S.dma('sp', ropeT[:, 1, :nt], dr['c_rope_s'][:, pos0:pos0 + nt])
            for l in range(nlayers):
                tg = '%s%d_L%d_' % (kind, pos0, l)
                S.tag = tg + 'a_s5'
                bmod[0] = 5
                norm_to_h(l, 0, nt)
                s5_phase(l, nt, kind)
                S.tag = tg + 'b_lru'
                bmod[0] = 8
                lru_phase(l, nt, kind)
                S.tag = tg + 'c_ret'
                bmod[0] = 5
                ret_phase(l, nt, kind)
                S.tag = tg + 'd_out'
                bmod[0] = 8
                outproj(l, nt)
                S.tag = tg + 'e_ffn'
                ffn_phase(l, nt, kind)
                S.tag = tg + 'f_post'
                if kind == 'P' and pos0 == 2048:
                    s5_store_prompt(l)
                    lru_store_prompt(l)
                    ret_store_prompt(l)
                    ffn_store_prompt(l)
            S.frozen = False
            if kind == 'P':
                L = min(nt, 128)
                for b in range(nt // L):
                    xt_ = xtok[xi[0] % 2]
                    xi[0] += 1
                    t0 = pos0 + b * L
                    tr_out(lambda cc, b=b, L=L: xT[:, cc, b * L:(b + 1) * L], xt_, L, 8)
                    if t0 == 0:
                        S.dma('pool', dr['y_prompt'][0:112, :], xt_[16:128, :])
                    else:
                        S.dma('pool', dr['y_prompt'][t0 - 16:t0 - 16 + L, :], xt_[0:L, :])
            else:
                xt_ = xtok[xi[0] % 2]
                xi[0] += 1
                tr_out(lambda cc: xT[:, cc, 0:64], xt_, 64, 8)
                S.dma('pool', dr['y_sample'], xt_[0:64, :])
        S.frozen = False
        if debug:
            S.dma('sp', dr['dbg'][:, 0:DEPTH * 8 * 24], s5c[:].rearrange("p l a k -> p (l a k)"))
            S.dma('sp', dr['dbg'][:, 1000:1000 + 120], lruc[:].rearrange("p l a k -> p (l a k)"))
        ncd.__exit__(None, None, None)
        global LAST_SCHED
        LAST_SCHED = S
        with nc.Block() as block:
            S.emit(block, sems, dsems)
    return nc


def make_in_maps(inputs):
    consts = host_consts()
    maps = []
    for c in range(8):
        m = {}
        m['x_prompt'] = np.ascontiguousarray(inputs['x_prompt'][c])
        m['x_sample'] = np.ascontiguousarray(inputs['x_sample'][c * 16:(c + 1) * 16]).reshape(64, D)
        m['state_s5_re'] = np.ascontiguousarray(inputs['state_s5_re'][:, c * 16:(c + 1) * 16]).reshape(DEPTH, 16, 3072)
        m['state_s5_im'] = np.ascontiguousarray(inputs['state_s5_im'][:, c * 16:(c + 1) * 16]).reshape(DEPTH, 16, 3072)
        m['state_lru'] = np.ascontiguousarray(inputs['state_lru'][:, c * 16:(c + 1) * 16])
        m['cache_lru_conv'] = np.ascontiguousarray(inputs['cache_lru_conv'][:, c * 16:(c + 1) * 16]).reshape(DEPTH, 48, 768)
        m['state_ret'] = np.ascontiguousarray(inputs['state_ret'][:, c * 16:(c + 1) * 16])
        m['cache_ffn_conv'] = np.ascontiguousarray(inputs['cache_ffn_conv'][:, c * 16:(c + 1) * 16]).reshape(DEPTH, 32, DFF)
        m['meta_tokens'] = np.ascontiguousarray(inputs['meta_tokens'])
        for n_ in WNAMES:
            m[n_] = np.ascontiguousarray(inputs[n_])
        for k_, v_ in consts.items():
            m['c_' + k_] = v_
        maps.append(m)
    return maps


_NC_CACHE = {}


def kernel(**inputs):
    inputs = {k: np.asarray(v) for k, v in inputs.items()}
    if 'nc' not in _NC_CACHE:
        _NC_CACHE['nc'] = build()
    nc = _NC_CACHE['nc']
    res = run_bass_kernel_spmd(nc, make_in_maps(inputs), core_ids=list(range(8)))
    R = res.results
    f = np.float32

    def cat(name, shape_per):
        return [np.asarray(R[c][name]) for c in range(8)]
    y_prompt = np.stack([R[c]['y_prompt'] for c in range(8)]).astype(f)
    y_sample = np.concatenate([np.asarray(R[c]['y_sample']).reshape(16, 4, D) for c in range(8)], 0).astype(f)
    p_s5_re = np.stack([np.asarray(R[c]['p_s5_re']).reshape(DEPTH, 48, 64) for c in range(8)], 1).astype(f)
    p_s5_im = np.stack([np.asarray(R[c]['p_s5_im']).reshape(DEPTH, 48, 64) for c in range(8)], 1).astype(f)
    p_lru_h = np.stack([np.asarray(R[c]['p_lru_h']).reshape(DEPTH, 768) for c in range(8)], 1).astype(f)
    p_lru_conv = np.stack([np.asarray(R[c]['p_lru_conv']).reshape(DEPTH, 3, 768) for c in range(8)], 1).astype(f)
    p_ret = np.stack([np.asarray(R[c]['p_ret']) for c in range(8)], 1).astype(f)
    p_ffn = np.stack([np.asarray(R[c]['p_ffn_conv']).reshape(DEPTH, 2, DFF) for c in range(8)], 1).astype(f)
    s_s5_re = np.concatenate([np.asarray(R[c]['s_s5_re']).reshape(DEPTH, 16, 48, 64) for c in range(8)], 1).astype(f)
    s_s5_im = np.concatenate([np.asarray(R[c]['s_s5_im']).reshape(DEPTH, 16, 48, 64) for c in range(8)], 1).astype(f)
    s_lru_h = np.concatenate([np.asarray(R[c]['s_lru_h']) for c in range(8)], 1).astype(f)
    s_lru_conv = np.concatenate([np.asarray(R[c]['s_lru_conv']).reshape(DEPTH, 16, 3, 768) for c in range(8)], 1).astype(f)
    s_ret = np.concatenate([np.asarray(R[c]['s_ret']) for c in range(8)], 1).astype(f)
    s_ffn = np.concatenate([np.asarray(R[c]['s_ffn_conv']).reshape(DEPTH, 16, 2, DFF) for c in range(8)], 1).astype(f)
    return (y_prompt, y_sample, p_s5_re, p_s5_im, p_lru_h, p_lru_conv, p_ret, p_ffn,
            s_s5_re, s_s5_im, s_lru_h, s_lru_conv, s_ret, s_ffn)
```

```python
import math
import numpy as np
import concourse.bass as bass
import concourse.mybir as mybir
from concourse.bass_utils import run_bass_kernel_spmd

F32 = mybir.dt.float32
BF16 = mybir.dt.bfloat16
AF = mybir.ActivationFunctionType
ALU = mybir.AluOpType
AX = mybir.AxisListType

D = 1024
DEPTH = 2
SEQ = 2048
NMETA = 16
DEC_B = 16
DEC_T = 4
PAST = 16384
DIN = 8448
DFF = 2816
EPS = 1e-6
NDS = 8


def dsize(dt):
    s = str(dt)
    if '32' in s:
        return 4
    if '16' in s:
        return 2
    if '64' in s:
        return 8
    return 1


class Sched:
    ENGS = ['pe', 'act', 'dve', 'pool', 'sp']

    def __init__(self, nc):
        self.nc = nc
        self.ops = {e: [] for e in self.ENGS}
        self.trk = {}
        self.dman = {e: 0 for e in self.ENGS}
        self.cut = None
        self.frozen = False

    def rect(self, ap):
        t = ap.tensor
        name = t.name
        pairs = ap.ap
        es = dsize(ap.dtype)
        off = int(ap.offset) * es
        sp = str(ap.space)
        if sp in ('SB', 'PSUM'):
            shp = list(t.shape)
            row = 1
            for s in shp[1:]:
                row *= s
            row *= dsize(t.dtype)
            p0 = off // row
            f0 = off % row
            pstep, pcnt = pairs[0]
            if pstep == 0:
                pcnt = 1
            ext = 0
            for st, cn in pairs[1:]:
                ext += abs(st) * (cn - 1)
            f1 = f0 + ext * es + es
            return name, p0, p0 + pcnt, f0, f1
        ext = 0
        for st, cn in pairs:
            ext += abs(st) * (cn - 1)
        return name, 0, 1, off, off + ext * es + es

    def mark(self, n):
        if self.cut is not None and n >= self.cut:
            self.frozen = True

    def add(self, eng, fn, reads, writes, dma=False):
        if self.frozen:
            return None
        idx = len(self.ops[eng])
        if dma:
            n = self.dman[eng]
            self.dman[eng] += 1
            me = ('dma', eng, n)
        else:
            me = (eng, idx)
        deps = set()
        if dma and n >= NDS:
            deps.add(('dma', eng, n - NDS))
        psum_aps = [ap for ap in list(reads) + list(writes) if str(ap.space) == 'PSUM']
        reads = [ap for ap in reads if str(ap.space) != 'PSUM']
        writes = [ap for ap in writes if str(ap.space) != 'PSUM']
        for ap in psum_aps:
            name, p0, p1, f0, f1 = self.rect(ap)
            p0 = (p0 // 32) * 32
            p1 = ((p1 + 31) // 32) * 32
            recs = self.trk.setdefault(name, [])
            keep = []
            for r in recs:
                if r[0] < p1 and p0 < r[1]:
                    if r[4] is not None:
                        deps.add(r[4])
                    if p0 <= r[0] and r[1] <= p1:
                        continue
                keep.append(r)
            keep.append([p0, p1, 0, 2048, me, {}])
            self.trk[name] = keep
        for ap in reads:
            name, p0, p1, f0, f1 = self.rect(ap)
            recs = self.trk.setdefault(name, [])
            found = None
            for r in recs:
                if r[0] < p1 and p0 < r[1] and r[2] < f1 and f0 < r[3]:
                    if r[4] is not None:
                        deps.add(r[4])
                    if r[0] == p0 and r[1] == p1 and r[2] == f0 and r[3] == f1 and r[4] is None:
                        found = r
            if found is None:
                found = [p0, p1, f0, f1, None, {}]
                recs.append(found)
            key = me if dma else eng
            found[5][key] = me
        for ap in writes:
            name, p0, p1, f0, f1 = self.rect(ap)
            recs = self.trk.setdefault(name, [])
            keep = []
            for r in recs:
                if r[0] < p1 and p0 < r[1] and r[2] < f1 and f0 < r[3]:
                    if r[4] is not None:
                        deps.add(r[4])
                    for v in r[5].values():
                        deps.add(v)
                    if p0 <= r[0] and r[1] <= p1 and f0 <= r[2] and r[3] <= f1:
                        continue
                keep.append(r)
            keep.append([p0, p1, f0, f1, me, {}])
            self.trk[name] = keep
        deps.discard(me)
        self.ops[eng].append(dict(fn=fn, deps=deps, me=me, dma=dma, sig=dma, tag=getattr(self, 'tag', '')))
        return me

    def tt(self, eng, out, in0, in1, op):
        self.add(eng, lambda e: e.tensor_tensor(out=out, in0=in0, in1=in1, op=op), [in0, in1], [out])

    def ts(self, eng, out, in0, s1, op0, s2=None, op1=None):
        rd = [in0] + [s for s in (s1, s2) if not isinstance(s, (int, float, type(None)))]
        if op1 is None:
            self.add(eng, lambda e: e.tensor_scalar(out=out, in0=in0, scalar1=s1, scalar2=None, op0=op0), rd, [out])
        else:
            self.add(eng, lambda e: e.tensor_scalar(out=out, in0=in0, scalar1=s1, scalar2=s2, op0=op0, op1=op1), rd, [out])

    def stt(self, out, in0, scalar, in1, op0, op1):
        rd = [in0, in1] + ([scalar] if not isinstance(scalar, (int, float)) else [])
        self.add('dve', lambda e: e.scalar_tensor_tensor(out=out, in0=in0, scalar=scalar, in1=in1, op0=op0, op1=op1), rd, [out])

    def act(self, out, in_, func, bias=None, scale=None):
        rd = [in_] + [s for s in (bias, scale) if not isinstance(s, (int, float, type(None)))]
        kw = {}
        if bias is not None:
            kw['bias'] = bias
        if scale is not None:
            kw['scale'] = scale
        self.add('act', lambda e: e.activation(out=out, in_=in_, func=func, **kw), rd, [out])

    def cp(self, eng, out, in_):
        if eng == 'act':
            self.add('act', lambda e: e.activation(out=out, in_=in_, func=AF.Identity), [in_], [out])
        else:
            self.add(eng, lambda e: e.tensor_copy(out=out, in_=in_), [in_], [out])

    def mm(self, out, lhsT, rhs, start, stop):
        self.add('pe', lambda e: e.matmul(out, lhsT, rhs, start=start, stop=stop), [lhsT, rhs], [out])

    def tr(self, out, in_, ident):
        self.add('pe', lambda e: e.transpose(out, in_, ident), [in_, ident], [out])

    def scan(self, out, d0, d1, init):
        rd = [d0, d1] + ([init] if not isinstance(init, (int, float)) else [])
        self.add('dve', lambda e: e.tensor_tensor_scan(out=out, data0=d0, data1=d1, initial=init, op0=ALU.mult, op1=ALU.add), rd, [out])

    def memset(self, eng, ap, val):
        self.add(eng, lambda e: e.memset(ap, val), [], [ap])

    def recip(self, out, in_):
        self.add('dve', lambda e: e.reciprocal(out=out, in_=in_), [in_], [out])

    def red(self, out, in_):
        self.add('dve', lambda e: e.tensor_reduce(out=out, in_=in_, axis=AX.X, op=ALU.add), [in_], [out])

    def dma(self, q, out, in_):
        self.add(q, lambda e: e.dma_start(out=out, in_=in_, allow_slow_non_contiguous=True), [in_], [out], dma=True)

    def emit(self, block, sems, dsems):
        ops = self.ops
        for e in self.ENGS:
            for op in ops[e]:
                for d in op['deps']:
                    if d[0] != 'dma':
                        ops[d[0]][d[1]]['sig'] = True
        cnt = {}
        for e in self.ENGS:
            c = 0
            for i, op in enumerate(ops[e]):
                if op['dma']:
                    continue
                if op['sig']:
                    c += 1
                cnt[(e, i)] = c
        finals = []
        for q in self.ENGS:
            n = self.dman[q]
            for s in range(min(NDS, n)):
                uses = (n - s + NDS - 1) // NDS
                finals.append((dsems[q][s], 16 * uses))

        def run(ename, eng):
            waited = {}
            for op in ops[ename]:
                for d in sorted(op['deps'], key=str):
                    if d[0] == 'dma':
                        sem = dsems[d[1]][d[2] % NDS]
                        val = 16 * (d[2] // NDS + 1)
                        key = ('d', d[1], d[2] % NDS)
                    else:
                        if d[0] == ename and ename == 'pe':
                            continue
                        if d[0] == ename and ename in ('dve', 'act') and (op['me'][1] - d[1]) >= 2:
                            continue
                        sem = sems[d[0]]
                        val = cnt[d]
                        key = d[0]
                    if waited.get(key, 0) >= val:
                        continue
                    waited[key] = val
                    eng.wait_ge(sem, val)
                inst = op['fn'](eng)
                if op['dma']:
                    inst.then_inc(dsems[ename][op['me'][2] % NDS], 16)
                elif op['sig']:
                    inst.then_inc(sems[ename], 1)
            if ename == 'sp':
                for sem, val in finals:
                    eng.wait_ge(sem, val)

        @block.sync
        def _(e):
            run('sp', e)

        @block.tensor
        def _(e):
            run('pe', e)

        @block.scalar
        def _(e):
            run('act', e)

        @block.vector
        def _(e):
            run('dve', e)

        @block.gpsimd
        def _(e):
            run('pool', e)


def piece_defs():
    P = []

    def add(name, src, r0, nr, c0, ncl):
        P.append((name, [(src, r0, nr, c0, ncl)]))

    for i in range(2):
        add('u%d' % i, 'w_in', 0, 1024, i * 384, 384)
    for i in range(2):
        add('glu%d' % i, 's5_w_glu', 0, 768, i * 384, 384)
    for i in range(2):
        add('wa%d' % i, 'w_branch_a', 0, 768, i * 512, 512)
    for i in range(2):
        add('ga%d' % i, 'w_in', 0, 1024, 5376 + i * 512, 512)
    for i in range(2):
        add('xb%d' % i, 'w_in', 0, 1024, 768 + i * 384, 384)
    for i in range(2):
        add('gb%d' % i, 'w_in', 0, 1024, 1536 + i * 384, 384)
    for i in range(2):
        add('wb%d' % i, 'w_branch_b', 0, 768, i * 512, 512)
    for i in range(2):
        add('gtb%d' % i, 'w_in', 0, 1024, 6400 + i * 512, 512)
    add('q', 'w_in', 0, 1024, 2304, 512)
    add('k', 'w_in', 0, 1024, 2816, 512)
    for i in range(2):
        add('v%d' % i, 'w_in', 0, 1024, 3328 + i * 512, 512)
    for i in range(2):
        add('gc%d' % i, 'w_in', 0, 1024, 4352 + i * 512, 512)
    for i in range(2):
        add('wc%d' % i, 'w_branch_c', 0, 1024, i * 512, 512)
    for i in range(2):
        add('gtc%d' % i, 'w_in', 0, 1024, 7424 + i * 512, 512)
    for i in range(2):
        add('wo%d' % i, 'w_out', 0, 1024, i * 512, 512)
    for j in range(11):
        P.append(('up%d' % j, [('ffn_w_up', 0, 1024, j * 256, 256), ('ffn_w_up', 0, 1024, DFF + j * 256, 256)]))
    for o in range(8):
        add('dn%d' % o, 'ffn_w_down', 0, 2816, o * 128, 128)
    return P


DEVP = [('B0', 3072), ('B1', 3072), ('C0', 3072), ('C1', 3072)] + [('T%d' % i, 4096) for i in range(6)] + [('LG', 1536)]

WNAMES = ['g_pre_mix', 'g_post_mix', 'g_pre_ffn', 'g_post_ffn', 'w_in', 's5_a_re', 's5_a_im', 's5_log_step',
          's5_b_re', 's5_b_im', 's5_c_re', 's5_c_im', 's5_d', 's5_w_glu', 's5_b_glu', 'lru_conv_w', 'lru_conv_b',
          'lru_w_a', 'lru_b_a', 'lru_w_x', 'lru_b_x', 'lru_lam', 'ret_norm_g', 'w_branch_a', 'w_branch_b',
          'w_branch_c', 'w_out', 'ffn_w_up', 'ffn_conv_w', 'ffn_conv_b', 'ffn_w_down']


def host_consts():
    c = {}
    c['ident'] = np.eye(128, dtype=np.float32)
    pm = np.zeros((128, 128), np.float32)
    for m in range(128):
        k = m + 32 if (m % 64) < 32 else m - 32
        pm[k, m] = 1.0
    c['perm'] = pm
    half = 32
    inv = np.power(np.float32(10000.0), -np.arange(half, dtype=np.float32) / np.float32(half)).astype(np.float32)
    pos = np.concatenate([np.arange(NMETA + SEQ), np.tile(PAST + np.arange(DEC_T), DEC_B)]).astype(np.float32)
    ang = (pos[:, None] * inv[None, :]).astype(np.float32)
    cs = np.cos(ang).astype(np.float32).T
    sn = np.sin(ang).astype(np.float32).T
    rc = np.zeros((128, pos.shape[0]), np.float32)
    rs = np.zeros((128, pos.shape[0]), np.float32)
    for p in range(128):
        f = p % 32
        rc[p] = cs[f]
        rs[p] = -sn[f] if (p % 64) < 32 else sn[f]
    c['rope_c'] = rc
    c['rope_s'] = rs
    lg = np.log1p(-np.exp2(-5.0 - np.arange(8, dtype=np.float32))).astype(np.float32)
    n = np.arange(128, dtype=np.float32)
    diff = n[None, :] - n[:, None]
    dt_ = np.where(diff >= 0, np.exp(lg[:, None, None] * np.maximum(diff, 0.0)[None]), 0.0).astype(np.float32)
    c['DT'] = np.ascontiguousarray((0.125 * dt_).transpose(1, 0, 2)).astype(np.float32)
    ms = np.arange(64)
    same = (ms[:, None] // 4 == ms[None, :] // 4) & (ms[None, :] >= ms[:, None])
    dts = np.where(same[None], np.exp(lg[:, None, None] * np.maximum(ms[None, :] - ms[:, None], 0)[None].astype(np.float32)), 0.0)
    c['DTS'] = np.ascontiguousarray((0.125 * dts).transpose(1, 0, 2)).astype(np.float32)
    gq = np.zeros((128, 4, 128), np.float32)
    gqs = np.zeros((128, 4, 64), np.float32)
    gC = np.zeros((128, 3, 4, 128), np.float32)
    for p in range(128):
        for cc in range(4):
            h = 2 * cc + p // 64
            gq[p, cc] = np.exp(lg[h] * (n + 1.0))
            gqs[p, cc] = np.exp(lg[h] * ((np.arange(64) % 4) + 1.0))
            gC[p, 0, cc] = np.exp(lg[h] * 128.0)
            gC[p, 1, cc] = np.exp(lg[h] * 16.0)
    c['GQ'] = gq
    c['GQS'] = gqs
    c['GC'] = gC
    dk = np.zeros((128, 3, 512), np.float32)
    for h in range(8):
        dk[:, 0, h * 64:(h + 1) * 64] = (0.125 * np.exp(lg[h] * (127.0 - n)))[:, None]
        dk[:16, 1, h * 64:(h + 1) * 64] = (0.125 * np.exp(lg[h] * (15.0 - n[:16])))[:, None]
        dk[:64, 2, h * 64:(h + 1) * 64] = (0.125 * np.exp(lg[h] * (3.0 - (np.arange(64) % 4))))[:, None]
    c['DK'] = dk
    c['G4'] = np.exp(lg * 4.0).astype(np.float32)
    mf = np.zeros((128, 16, 64), np.float32)
    for b in range(16):
        mf[:, b, 4 * b:4 * b + 4] = 1.0
    c['MF'] = mf
    mt = np.zeros((64, 16), np.float32)
    for b in range(16):
        mt[4 * b:4 * b + 4, b] = 1.0
    c['MT'] = np.ascontiguousarray(np.pad(mt, ((0, 64), (0, 0))))
    zm = np.ones((128, 64), np.float32)
    zm[:, 0::4] = 0.0
    c['ZM'] = zm
    c['IOTA'] = np.tile(np.arange(1, 257, dtype=np.float32)[None], (128, 1))
    return c


CONST_SHAPES = None
LAST_SCHED = None


def build(debug=None):
    consts = host_consts()
    nc = bass.Bass("TRN2", target_bir_lowering=False)
    S = Sched(nc)
    if debug and 'cut' in debug:
        S.cut = int(debug.split('cut')[1])
    dr = {}

    def din(name, shape, dt=F32):
        dr[name] = nc.dram_tensor(name, list(shape), dt, kind="ExternalInput").ap()
        return dr[name]

    def dout(name, shape, dt=F32):
        dr[name] = nc.dram_tensor(name, list(shape), dt, kind="ExternalOutput").ap()
        return dr[name]

    din('x_prompt', [SEQ, D]); din('x_sample', [DEC_B * DEC_T, D])
    din('state_s5_re', [DEPTH, DEC_B, 3072]); din('state_s5_im', [DEPTH, DEC_B, 3072])
    din('state_lru', [DEPTH, DEC_B, 768]); din('cache_lru_conv', [DEPTH, DEC_B * 3, 768])
    din('state_ret', [DEPTH, DEC_B, 8, 64, 128]); din('cache_ffn_conv', [DEPTH, DEC_B * 2, DFF])
    din('meta_tokens', [NMETA, D])
    shapes = dict(g_pre_mix=[2, 1024], g_post_mix=[2, 1024], g_pre_ffn=[2, 1024], g_post_ffn=[2, 1024],
                  w_in=[2, 1024, DIN], s5_a_re=[2, 48, 64], s5_a_im=[2, 48, 64], s5_log_step=[2, 48],
                  s5_b_re=[2, 48, 64, 16], s5_b_im=[2, 48, 64, 16], s5_c_re=[2, 48, 16, 64], s5_c_im=[2, 48, 16, 64],
                  s5_d=[2, 768], s5_w_glu=[2, 768, 768], s5_b_glu=[2, 768], lru_conv_w=[2, 4, 768], lru_conv_b=[2, 768],
                  lru_w_a=[2, 12, 64, 64], lru_b_a=[2, 768], lru_w_x=[2, 12, 64, 64], lru_b_x=[2, 768], lru_lam=[2, 768],
                  ret_norm_g=[2, 1024], w_branch_a=[2, 768, 1024], w_branch_b=[2, 768, 1024], w_branch_c=[2, 1024, 1024],
                  w_out=[2, 1024, 1024], ffn_w_up=[2, 1024, 2 * DFF], ffn_conv_w=[2, 3, DFF], ffn_conv_b=[2, DFF],
                  ffn_w_down=[2, DFF, 1024])
    for n_ in WNAMES:
        din(n_, shapes[n_])
    for k_, v_ in consts.items():
        din('c_' + k_, v_.shape)
    dout('y_prompt', [SEQ, D]); dout('y_sample', [64, D])
    dout('p_s5_re', [DEPTH, 24, 128]); dout('p_s5_im', [DEPTH, 24, 128]); dout('p_lru_h', [DEPTH, 6, 128])
    dout('p_lru_conv', [DEPTH, 3, 6, 128]); dout('p_ret', [DEPTH, 8, 64, 128]); dout('p_ffn_conv', [DEPTH, 2, 22, 128])
    dout('s_s5_re', [DEPTH, DEC_B, 3072]); dout('s_s5_im', [DEPTH, DEC_B, 3072]); dout('s_lru_h', [DEPTH, DEC_B, 768])
    dout('s_lru_conv', [DEPTH, DEC_B * 3, 768]); dout('s_ret', [DEPTH, DEC_B, 8, 64, 128]); dout('s_ffn_conv', [DEPTH, DEC_B * 2, DFF])
    if debug:
        dout('dbg', [128, 8192])

    pdefs = piece_defs()
    poff = {}
    tot = 0
    for l in range(DEPTH):
        for name, srcs in pdefs:
            nr = srcs[0][2]
            E = (nr // 128) * sum(s[4] for s in srcs)
            poff[(l, name)] = (tot, E)
            tot += 128 * E
        for name, E in DEVP:
            poff[(l, name)] = (tot, E)
            tot += 128 * E
    wscr = nc.dram_tensor('wscr', [tot], BF16, kind="Internal").ap()

    def pview(l, name):
        o, E = poff[(l, name)]
        return wscr[o:o + 128 * E].rearrange("(p e) -> p e", p=128), E

    import contextlib
    es = contextlib.ExitStack()
    with es:
        def sb(name, shape, dt=F32):
            return es.enter_context(nc.sbuf_tensor(name, list(shape), dt))

        sems = {e: es.enter_context(nc.semaphore('s_' + e)) for e in ['pe', 'act', 'dve', 'pool']}
        dsems = {q: [es.enter_context(nc.semaphore('d_%s%d' % (q, i))) for i in range(NDS)] for q in Sched.ENGS}
        banks = [es.enter_context(nc.psum_tensor('ps%d' % i, [128, 512], F32)) for i in range(8)]
        bctr = [0]

        def bank():
            b = banks[bctr[0] % 8]
            bctr[0] += 1
            return b

        NRING = 5
        ring = [sb('ring%d' % i, [128, 4096], BF16) for i in range(NRING)]
        rctr = [0]
        ident = sb('ident', [128, 128]); perm = sb('perm', [128, 128]); ones = sb('ones', [128, 128])
        identb = sb('identb', [128, 128], BF16)
        negI = sb('negI', [128, 128])
        xT = sb('xT', [128, 8, 512]); hT = sb('hT', [128, 8, 512], BF16); mA = sb('mA', [128, 8, 512])
        rstd = sb('rstd', [128, 512]); sqt = [sb('sqt%d' % i, [128, 512]) for i in range(2)]
        gv = sb('gv', [128, DEPTH, 4, 8])
        ropeT = sb('ropeT', [128, 2, 512])
        cDT = sb('cDT', [128, 8, 128]); cDTS = sb('cDTS', [64, 8, 64]); cGQ = sb('cGQ', [128, 4, 128]); cGQS = sb('cGQS', [128, 4, 64])
        cGC = sb('cGC', [128, 3, 4, 128]); cDK = sb('cDK', [128, 3, 512]); cMF = sb('cMF', [128, 16, 64], BF16)
        cMT = sb('cMT', [128, 16]); cZM = sb('cZM', [128, 64])
        s5c = sb('s5c', [128, DEPTH, 8, 24]); s5c2 = sb('s5c2', [128, DEPTH, 24]); s5d = sb('s5d', [128, DEPTH, 2, 6])
        lruc = sb('lruc', [128, DEPTH, 10, 6]); ffnc = sb('ffnc', [128, DEPTH, 4, 22])
        st_s5 = sb('st_s5', [128, DEPTH, 2, 24]); st_lru = sb('st_lru', [128, DEPTH, 6]); st_lc = sb('st_lc', [128, DEPTH, 6, 3])
        st_ff = sb('st_ff', [128, DEPTH, 22, 2]); st_ret = sb('st_ret', [128, DEPTH, 4, 128]); st_retb = sb('st_retb', [128, DEPTH, 4, 128], BF16)
        yT = sb('yT', [128, 8, 512], BF16)
        gt = [sb('gt%d' % i, [128, 512]) for i in range(2)]
        fixb = sb('fixb', [128, 24, 2, 5]); stat = sb('stat', [128, 6, 8])
        WKB = 56 * 1024
        work = sb('work', [128, WKB // 2], BF16)

        class Carver:
            def __init__(self):
                self.o = 0

            def get(self, shape, dt=F32):
                Carver.mx = max(getattr(Carver, 'mx', 0), self.o)
                n = 1
                for s_ in shape[1:]:
                    n *= s_
                nb = n * dsize(dt)
                nb = (nb + 63) // 64 * 64
                assert self.o + nb <= WKB, (self.o, nb)
                v = work[0:shape[0], self.o // 2:(self.o + nb) // 2]
                if dt != BF16:
                    v = v.bitcast(dt)
                v = v[:, 0:n]
                self.o += nb
                if len(shape) == 3:
                    v = v.rearrange("p (a b) -> p a b", a=shape[1])
                elif len(shape) == 4:
                    v = v.rearrange("p (a b c) -> p a b c", a=shape[1], b=shape[2])
                return v

        cv = Carver()
        big = sb('big', [128, 3072]); sstg = cv.get([128, 3072], BF16); xtok = [cv.get([128, 1024]) for _ in range(2)]; aTs = cv.get([128, 3072])
        cv = Carver()
        uf = [cv.get([128, 512]) for _ in range(2)]; ub = [cv.get([128, 512], BF16) for _ in range(2)]
        NSL = 2
        wk = [[cv.get([128, 512]) for _ in range(6)] for s_ in range(NSL)]
        hb = [[cv.get([128, 512], BF16) for _ in range(2)] for s_ in range(NSL)]
        ytmp = [cv.get([128, 512]) for _ in range(2)]
        zT = cv.get([128, 6, 512], BF16)
        h0s = cv.get([128, 2, 24, 16]); tny = cv.get([128, 64]); st3t = [cv.get([128, 512]) for _ in range(4)]
        cv = Carver()
        xbe = cv.get([128, 6, 520]); lw = [cv.get([128, 512]) for _ in range(16)]; lwb = [cv.get([128, 512], BF16) for _ in range(4)]
        h0l = cv.get([128, 6, 16])
        cv = Carver()
        vtok = cv.get([128, 4, 1024], BF16); kdec = cv.get([128, 4, 512], BF16)
        qT = cv.get([128, 4, 512], BF16); kT = cv.get([128, 4, 512], BF16); qg = cv.get([128, 4, 512], BF16)
        gsn = cv.get([128, 1024]); otm = cv.get([128, 1024]); yct = cv.get([128, 1024], BF16)
        PT = [cv.get([128, 4, 128], BF16) for _ in range(2)]
        qgm = cv.get([128, 16, 64], BF16); vm = cv.get([128, 8, 128], BF16)
        srs = [cv.get([128, 8, 128])]; srb = cv.get([128, 8, 128], BF16)
        normg = cv.get([128, 1024]); rw = [cv.get([128, 512]) for _ in range(3)]
        cv = Carver()
        aT = cv.get([128, 22, 512], BF16); gfe = [cv.get([128, 576]) for _ in range(4)]; fw = [cv.get([128, 512]) for _ in range(8)]; h0f = cv.get([128, 22, 32])

        def ringslot():
            r = ring[rctr[0] % NRING]
            rctr[0] += 1
            return r

        def piece(l, name, dt=BF16, slot=None):
            src, E = pview(l, name)
            r = ringslot() if slot is None else ring[slot]
            S.dma('sp', r[:, 0:E], src)
            if dt == F32:
                return r[:, 0:E].bitcast(F32)
            return r[:, 0:E]

        ncd = nc.allow_non_contiguous_dma(reason="small strided param loads")
        ncd.__enter__()
        S.mark(0)
        S.dma('sp', ident[:], dr['c_ident']); S.dma('sp', perm[:], dr['c_perm'])
        S.memset('dve', ones[:], 1.0)
        S.cp('dve', identb[:], ident[:])
        S.ts('dve', negI[:], ident[:], -1.0, ALU.mult)
        S.dma('sp', cDT[:], dr['c_DT']); S.dma('sp', cDTS[:], dr['c_DTS']); S.dma('sp', cGQ[:], dr['c_GQ']); S.dma('sp', cGQS[:], dr['c_GQS'])
        S.dma('sp', cGC[:], dr['c_GC']); S.dma('sp', cDK[:], dr['c_DK']); S.dma('sp', cZM[:], dr['c_ZM']); S.dma('sp', cMT[:], dr['c_MT'])
        S.dma('pool', cMF[:], dr['c_MF'])
        S.mark(10)
        for l in range(DEPTH):
            for i, nm in enumerate(['g_pre_mix', 'g_post_mix', 'g_pre_ffn', 'g_post_ffn']):
                S.dma('sp', gv[:, l, i, :], dr[nm][l].rearrange("(c p) -> p c", p=128))
            S.dma('sp', s5d[:, l, 0, :], dr['s5_d'][l].rearrange("(c p) -> p c", p=128))
            S.dma('sp', s5d[:, l, 1, :], dr['s5_b_glu'][l].rearrange("(c p) -> p c", p=128))
            for j in range(4):
                S.dma('sp', lruc[:, l, j, :], dr['lru_conv_w'][l, j].rearrange("(c p) -> p c", p=128))
            for j, nm in enumerate(['lru_conv_b', 'lru_b_a', 'lru_b_x', 'lru_lam']):
                S.dma('sp', lruc[:, l, 4 + j, :], dr[nm][l].rearrange("(c p) -> p c", p=128))
            for j in range(3):
                S.dma('sp', ffnc[:, l, j, :], dr['ffn_conv_w'][l, j].rearrange("(c p) -> p c", p=128))
            S.dma('sp', ffnc[:, l, 3, :], dr['ffn_conv_b'][l].rearrange("(c p) -> p c", p=128))
            S.mark(11)
            S.act(lruc[:, l, 8, :], lruc[:, l, 7, :], AF.Exp, scale=-1.0)
            S.act(lruc[:, l, 9, :], lruc[:, l, 8, :], AF.Ln, bias=1.0)
            S.ts('dve', lruc[:, l, 7, :], lruc[:, l, 9, :], -8.0, ALU.mult)
            S.ts('dve', lruc[:, l, 8, :], lruc[:, l, 9, :], -16.0, ALU.mult)
        S.memset('dve', st_s5[:], 0.0); S.memset('dve', st_lru[:], 0.0); S.memset('dve', st_lc[:], 0.0)
        S.memset('dve', st_ff[:], 0.0); S.memset('dve', st_ret[:], 0.0); S.memset('dve', st_retb[:], 0.0)

        S.mark(1)
        for l in range(DEPTH):
            for name, srcs in pdefs:
                dst, E = pview(l, name)
                kc = srcs[0][2] // 128
                wtot = sum(s[4] for s in srcs)
                d3 = dst.rearrange("p (k n) -> p k n", k=kc)
                c0 = 0
                for (sn_, r0, nr, cc0, ncl) in srcs:
                    src = dr[sn_][l][r0:r0 + nr, cc0:cc0 + ncl].rearrange("(k p) n -> p k n", p=128)
                    if kc > 8:
                        for k0 in range(0, kc, 8):
                            k1 = min(kc, k0 + 8)
                            S.dma('pool', d3[:, k0:k1, c0:c0 + ncl], src[:, k0:k1, :])
                    else:
                        S.dma('pool', d3[:, :, c0:c0 + ncl], src)
                    c0 += ncl

        S.mark(2)
        TWO_PI = 2.0 * math.pi

        MAGIC = 12582912.0

        def sincos(dst_c, dst_s, ang, tmp1, tmp2, tmp3):
            for (dst, shift) in ((dst_s, 0.0), (dst_c, 0.25)):
                S.ts('dve', tmp1, ang, 1.0 / TWO_PI, ALU.mult, shift, ALU.add)
                S.ts('dve', tmp2, tmp1, MAGIC, ALU.add)
                S.ts('dve', tmp3, tmp2, MAGIC, ALU.subtract)
                S.tt('dve', tmp1, tmp1, tmp3, ALU.subtract)
                S.act(dst, tmp1, AF.Sin, scale=TWO_PI)

        for l in range(DEPTH):
            A = big
            are = A[:, 0:24]; aim = A[:, 24:48]; ls = A[:, 48:72]; dtv = A[:, 72:96]; ang = A[:, 96:120]
            t1 = A[:, 120:144]; t2 = A[:, 144:168]; t3 = A[:, 168:192]; abr = A[:, 192:216]; abi = A[:, 216:240]
            crr = A[:, 240:264]; cii = A[:, 264:288]; den = A[:, 288:312]; nr_ = A[:, 312:336]
            S.dma('sp', are, dr['s5_a_re'][l].rearrange("(k two) p -> (two p) k", two=2))
            S.dma('sp', aim, dr['s5_a_im'][l].rearrange("(k two) p -> (two p) k", two=2))
            lsv = dr['s5_log_step'][l].rearrange("(k two) -> two k", two=2)
            S.dma('sp', ls[0:64, :], lsv[0].partition_broadcast(64))
            S.dma('sp', ls[64:128, :], lsv[1].partition_broadcast(64))
            S.act(dtv, ls, AF.Exp)
            S.tt('dve', t1, are, dtv, ALU.mult)
            S.act(s5c[:, l, 0, :], t1, AF.Exp)
            S.tt('dve', ang, aim, dtv, ALU.mult)
            sincos(s5c[:, l, 1, :], s5c[:, l, 2, :], ang, t1, t2, t3)
            S.tt('dve', abr, s5c[:, l, 0, :], s5c[:, l, 1, :], ALU.mult)
            S.tt('dve', abi, s5c[:, l, 0, :], s5c[:, l, 2, :], ALU.mult)
            S.tt('dve', den, are, are, ALU.mult); S.tt('dve', t1, aim, aim, ALU.mult); S.tt('dve', den, den, t1, ALU.add)
            S.recip(den, den)
            S.ts('dve', nr_, abr, -1.0, ALU.add)
            S.tt('dve', t1, nr_, are, ALU.mult); S.tt('dve', t2, abi, aim, ALU.mult); S.tt('dve', t1, t1, t2, ALU.add)
            S.tt('dve', crr, t1, den, ALU.mult)
            S.tt('dve', t1, abi, are, ALU.mult); S.tt('dve', t2, nr_, aim, ALU.mult); S.tt('dve', t1, t1, t2, ALU.subtract)
            S.tt('dve', cii, t1, den, ALU.mult)
            S.mark(3)
            for (n_, ci_, si_) in ((256.0, s5c[:, l, 3, :], s5c[:, l, 4, :]), (16.0, s5c[:, l, 5, :], s5c[:, l, 6, :]), (4.0, s5c[:, l, 7, :], s5c2[:, l, :])):
                S.ts('dve', A[:, 336:360], ang, n_, ALU.mult)
                sincos(ci_, si_, A[:, 336:360], t1, t2, t3)
            iot = A[:, 384:640]
            S.dma('sp', iot, dr['c_IOTA'])
            for pc in range(6):
                tb = A[:, 1024:3072].rearrange("p (k r t) -> p k r t", k=4, r=2)
                angt = aTs[:, 0:256]; q1 = aTs[:, 256:512]; q2 = aTs[:, 512:768]; q3 = aTs[:, 768:1024]
                for kk in range(4):
                    k = pc * 4 + kk
                    S.ts('dve', angt, iot, ang[:, k:k + 1], ALU.mult)
                    sincos(tb[:, kk, 0, :], tb[:, kk, 1, :], angt, q1, q2, q3)
                dst, E = pview(l, 'T%d' % pc)
                S.dma('sp', dst.bitcast(F32), A[:, 1024:3072])
            S.mark(4)
            bre = A[:, 1024:1408].rearrange("p (k i) -> p k i", k=24); bim = A[:, 1408:1792].rearrange("p (k i) -> p k i", k=24)
            Bre = A[:, 1792:2176].rearrange("p (k i) -> p k i", k=24); Bim = A[:, 2176:2560].rearrange("p (k i) -> p k i", k=24)
            tq = A[:, 2560:2944].rearrange("p (k i) -> p k i", k=24)
            S.dma('sp', bre, dr['s5_b_re'][l].rearrange("(k two) p i -> (two p) k i", two=2))
            S.dma('sp', bim, dr['s5_b_im'][l].rearrange("(k two) p i -> (two p) k i", two=2))
            crb = crr.unsqueeze(2).to_broadcast([128, 24, 16]); cib = cii.unsqueeze(2).to_broadcast([128, 24, 16])
            S.tt('dve', Bre, bre, crb, ALU.mult); S.tt('dve', tq, bim, cib, ALU.mult); S.tt('dve', Bre, Bre, tq, ALU.subtract)
            S.tt('dve', Bim, bim, crb, ALU.mult); S.tt('dve', tq, bre, cib, ALU.mult); S.tt('dve', Bim, Bim, tq, ALU.add)
            stg = xtok[0][:, 0:256].rearrange("p (r c) -> p r c", r=2)
            for half in range(2):
                bo = sstg
                bo4 = bo.rearrange("p (k r s) -> p k r s", k=12, r=2)
                for kk in range(12):
                    k = half * 12 + kk
                    q = k % 4
                    S.memset('pool', stg, 0.0)
                    for r_, Bsrc in enumerate((Bre, Bim)):
                        S.cp('pool', stg[0:64, r_, 32 * q:32 * q + 16], Bsrc[0:64, k, :])
                        S.cp('pool', stg[64:128, r_, 32 * q + 16:32 * q + 32], Bsrc[64:128, k, :])
                    pb = bank()
                    for r_ in range(2):
                        S.tr(pb[:, r_ * 128:(r_ + 1) * 128], stg[:, r_, :], ident[:])
                    S.cp('act', bo4[:, kk, :, :], pb[:, 0:256].rearrange("p (r s) -> p r s", r=2))
                dst, E = pview(l, 'B%d' % half)
                S.dma('sp', dst, bo)
            S.mark(5)
            Zr = A[:, 0:3072].rearrange("p (k s) -> p k s", k=24)
            Zi = aTs.rearrange("p (k s) -> p k s", k=24)
            S.memset('pool', A[:, 0:3072], 0.0)
            S.memset('pool', aTs, 0.0)
            for Zz, nm in ((Zr, 's5_c_re'), (Zi, 's5_c_im')):
                cv = dr[nm][l].rearrange("(c q two) i p -> q two i c p", q=4, two=2)
                for q in range(4):
                    for two in range(2):
                        S.dma('sp', Zz[32 * q + 16 * two:32 * q + 16 * two + 16, q::4, two * 64:two * 64 + 64], cv[q, two])
            for half in range(2):
                cob = sstg
                co = cob.rearrange("p (k r s) -> p k r s", k=12, r=2)
                for kk in range(12):
                    k = half * 12 + kk
                    pb = bank()
                    S.tr(pb[:, 0:128], Zr[:, k, :], ident[:])
                    S.tr(pb[:, 128:256], Zi[:, k, :], ident[:])
                    S.cp('act', co[:, kk, 0, :], pb[:, 0:128])
                    S.act(co[:, kk, 1, :], pb[:, 128:256], AF.Identity, scale=-1.0)
                dst, E = pview(l, 'C%d' % half)
                S.dma('sp', dst, cob)
            S.mark(6)
            lgf = A[:, 0:1536].rearrange("p (g c s) -> p g c s", g=2, c=6)
            S.memset('pool', A[:, 0:1536], 0.0)
            for g_, nm in enumerate(('lru_w_a', 'lru_w_x')):
                wv = dr[nm][l].rearrange("(c two) i j -> two i c j", two=2)
                S.dma('sp', lgf[0:64, g_, :, 0:64], wv[0])
                S.dma('sp', lgf[64:128, g_, :, 64:128], wv[1])
            lgb = sstg[:, 0:1536]
            S.cp('dve', lgb, A[:, 0:1536])
            dst, E = pview(l, 'LG')
            S.dma('sp', dst, lgb)
            S.mark(7)

        G4 = [float(x) for x in consts['G4']]

        def W3(ap, kc):
            return ap.rearrange("p (k n) -> p k n", k=kc)

        lbank = [banks[5], banks[6], banks[7]]

        bmod = [5]

        def bank():
            b = banks[bctr[0] % bmod[0]]
            bctr[0] += 1
            return b

        def tr_in(dst_fn, src_tok, rows, nchunks):
            for c0 in range(0, nchunks, 4):
                n = min(4, nchunks - c0)
                pb = bank()
                for j in range(n):
                    S.tr(pb[:, j * rows:(j + 1) * rows], src_tok[0:rows, (c0 + j) * 128:(c0 + j + 1) * 128], ident[0:rows, 0:rows])
                S.cp('act', dst_fn(c0, n), pb[:, 0:n * rows].rearrange("p (j r) -> p j r", j=n))

        def tr_out(src_fn, dst_tok, rows, nchunks):
            for c0 in range(0, nchunks, 4):
                n = min(4, nchunks - c0)
                pb = bank()
                for j in range(n):
                    S.tr(pb[0:rows, j * 128:(j + 1) * 128], src_fn(c0 + j), ident[:])
                S.cp('act', dst_tok[0:rows, c0 * 128:(c0 + n) * 128], pb[0:rows, 0:n * 128])

        def rms_rstd(src3, nt):
            pb = bank()
            for c in range(8):
                sq = sqt[c % 2]
                S.act(sq[:, :nt], src3[:, c, :nt], AF.Square)
                S.mm(pb[:, :nt], ones[:], sq[:, :nt], start=(c == 0), stop=(c == 7))
            S.act(rstd[:, :nt], pb[:, :nt], AF.Sqrt, bias=EPS, scale=1.0 / 1024)
            S.recip(rstd[:, :nt], rstd[:, :nt])

        def norm_to_h(l, gi, nt):
            rms_rstd(xT, nt)
            for c in range(8):
                S.stt(hT[:, c, :nt], xT[:, c, :nt], gv[:, l, gi, c:c + 1], rstd[:, :nt], ALU.mult, ALU.mult)

        def resid_add(l, gi, src3, nt):
            rms_rstd(src3, nt)
            for c in range(8):
                S.stt(src3[:, c, :nt], src3[:, c, :nt], gv[:, l, gi, c:c + 1], rstd[:, :nt], ALU.mult, ALU.mult)
                S.tt('pool', xT[:, c, :nt], xT[:, c, :nt], src3[:, c, :nt], ALU.add)

        def branch(l, ysrc, kcn, wn, gn, nt, first):
            for half in range(2):
                Wp = W3(piece(l, wn + str(half)), kcn)
                Gp = W3(piece(l, gn + str(half)), 8)
                for oc4 in range(4):
                    oc = half * 4 + oc4
                    p1 = bank()
                    p2 = bank()
                    for kc in range(kcn):
                        S.mm(p1[:, :nt], Wp[:, kc, oc4 * 128:(oc4 + 1) * 128], ysrc[:, kc, :nt], start=(kc == 0), stop=(kc == kcn - 1))
                    for kc in range(8):
                        S.mm(p2[:, :nt], Gp[:, kc, oc4 * 128:(oc4 + 1) * 128], hT[:, kc, :nt], start=(kc == 0), stop=(kc == 7))
                    g = gt[oc % 2]
                    S.act(g[:, :nt], p2[:, :nt], AF.Sigmoid)
                    if first:
                        S.tt('dve', mA[:, oc, :nt], p1[:, :nt], g[:, :nt], ALU.mult)
                    else:
                        S.tt('dve', g[:, :nt], p1[:, :nt], g[:, :nt], ALU.mult)
                        S.tt('pool', mA[:, oc, :nt], mA[:, oc, :nt], g[:, :nt], ALU.add)

        def s5_phase(l, nt, kind):
            if kind == 'S':
                nseg, L = 16, 4
                cN, sN = s5c[:, l, 7, :], s5c2[:, l, :]
                for r_, nm in enumerate(('state_s5_re', 'state_s5_im')):
                    S.dma('sp', big[0:16, :], dr[nm][l])
                    for c0 in range(0, 24, 8):
                        pb = bank()
                        for j in range(8):
                            S.tr(pb[:, j * 16:(j + 1) * 16], big[0:16, (c0 + j) * 128:(c0 + j + 1) * 128], ident[0:16, 0:16])
                        S.cp('act', h0s[:, r_, c0:c0 + 8, :], pb[:, 0:128].rearrange("p (j r) -> p j r", j=8))
            else:
                L = min(nt, 256)
                nseg = nt // L
                cN, sN = (s5c[:, l, 3, :], s5c[:, l, 4, :]) if L == 256 else (s5c[:, l, 5, :], s5c[:, l, 6, :])

            def seg3(ap):
                return ap[:, :nt].rearrange("p (s t) -> p s t", s=nseg)
            stt_ = {}
            ctx = {}

            def st1(k):
                c, q = divmod(k, 4)
                if k % 12 == 0:
                    stt_['U'] = W3(piece(l, 'u%d' % (k // 12), slot=0), 8)
                    stt_['B'] = piece(l, 'B%d' % (k // 12), slot=1).rearrange("p (k r s) -> p k r s", k=12, r=2)
                if k % 4 == 0:
                    stt_['T'] = piece(l, 'T%d' % (k // 4), F32, slot=3 + (k // 4) % 2).rearrange("p (k r t) -> p k r t", k=4, r=2)
                ufc = uf[c % 2]
                ubc = ub[c % 2]
                if q == 0:
                    pu = bank()
                    for kc in range(8):
                        S.mm(pu[:, :nt], stt_['U'][:, kc, (c % 3) * 128:(c % 3 + 1) * 128], hT[:, kc, :nt], start=(kc == 0), stop=(kc == 7))
                    S.cp('act', ufc[:, :nt], pu[:, :nt])
                    S.cp('act', ubc[:, :nt], ufc[:, :nt])
                sl = k % NSL
                w = wk[sl]
                Bk = stt_['B'][:, k % 12]
                Tk = stt_['T'][:, k % 4]
                pbr = bank()
                pbi = bank()
                S.mm(pbr[:, :nt], Bk[:, 0, :], ubc[:, :nt], True, True)
                S.mm(pbi[:, :nt], Bk[:, 1, :], ubc[:, :nt], True, True)
                cb = Tk[:, 0, 0:L].unsqueeze(1).to_broadcast([128, nseg, L])
                sb_ = Tk[:, 1, 0:L].unsqueeze(1).to_broadcast([128, nseg, L])
                S.tt('dve', seg3(w[0]), seg3(pbr), cb, ALU.mult)
                S.tt('dve', seg3(w[1]), seg3(pbi), sb_, ALU.mult)
                S.tt('dve', seg3(w[2]), seg3(pbi), cb, ALU.mult)
                S.tt('dve', seg3(w[3]), seg3(pbr), sb_, ALU.mult)
                S.mm(pbr[:, :nt], ident[:], w[0][:, :nt], True, False)
                S.mm(pbr[:, :nt], ident[:], w[1][:, :nt], False, True)
                S.mm(pbi[:, :nt], ident[:], w[2][:, :nt], True, False)
                S.mm(pbi[:, :nt], negI[:], w[3][:, :nt], False, True)
                ctx[k] = (w, cb, sb_, sl, pbr, pbi)

            def st2(k):
                w, cb, sb_, sl, pwr, pwi = ctx[k]
                magk = s5c[:, l, 0, k:k + 1]
                if kind == 'P':
                    magb = magk.to_broadcast([128, L])
                    for s_ in range(nseg):
                        ire = st_s5[:, l, 0, k:k + 1] if s_ == 0 else fixb[:, k, 0, s_ - 1:s_]
                        iim = st_s5[:, l, 1, k:k + 1] if s_ == 0 else fixb[:, k, 1, s_ - 1:s_]
                        last = s_ == nseg - 1
                        dre = st_s5[:, l, 0, k:k + 1] if last else fixb[:, k, 0, s_:s_ + 1]
                        dim_ = st_s5[:, l, 1, k:k + 1] if last else fixb[:, k, 1, s_:s_ + 1]
                        gre = w[4][:, (s_ + 1) * L - 1:(s_ + 1) * L]
                        gim = w[5][:, (s_ + 1) * L - 1:(s_ + 1) * L]
                        t0 = tny[:, 4 * (k % 8):4 * (k % 8) + 1]
                        t1_ = tny[:, 4 * (k % 8) + 1:4 * (k % 8) + 2]

                        def sc_re():
                            S.scan(w[4][:, s_ * L:(s_ + 1) * L], magb, pwr[:, s_ * L:(s_ + 1) * L], ire)

                        def sc_im():
                            S.scan(w[5][:, s_ * L:(s_ + 1) * L], magb, pwi[:, s_ * L:(s_ + 1) * L], iim)

                        def f_t0():
                            S.ts('dve', t0, gim, sN[:, k:k + 1], ALU.mult)

                        def f_t1():
                            S.ts('dve', t1_, gre, sN[:, k:k + 1], ALU.mult)

                        def f_dre():
                            S.stt(dre, gre, cN[:, k:k + 1], t0, ALU.mult, ALU.subtract)

                        def f_dim():
                            S.stt(dim_, gim, cN[:, k:k + 1], t1_, ALU.mult, ALU.add)
                        if s_ % 2 == 0:
                            for f_ in (sc_re, sc_im, f_t1, f_t0, f_dim, f_dre):
                                f_()
                        else:
                            for f_ in (sc_im, sc_re, f_t0, f_t1, f_dre, f_dim):
                                f_()
                else:
                    wre3 = pwr[:, :64].rearrange("p (b t) -> p b t", t=4)
                    wim3 = pwi[:, :64].rearrange("p (b t) -> p b t", t=4)
                    S.stt(wre3[:, :, 0], h0s[:, 0, k, :], magk, wre3[:, :, 0], ALU.mult, ALU.add)
                    S.stt(wim3[:, :, 0], h0s[:, 1, k, :], magk, wim3[:, :, 0], ALU.mult, ALU.add)
                    S.ts('dve', w[1][:, :64], cZM[:, :], magk, ALU.mult)
                    S.scan(w[4][:, :64], w[1][:, :64], pwr[:, :64], 0.0)
                    S.scan(w[5][:, :64], w[1][:, :64], pwi[:, :64], 0.0)
                    g3r = w[4][:, :64].rearrange("p (b t) -> p b t", t=4)[:, :, 3]
                    g3i = w[5][:, :64].rearrange("p (b t) -> p b t", t=4)[:, :, 3]
                    S.ts('dve', tny[:, 0:16], g3i, sN[:, k:k + 1], ALU.mult)
                    S.ts('dve', tny[:, 16:32], g3r, sN[:, k:k + 1], ALU.mult)
                    S.stt(h0s[:, 0, k, :], g3r, cN[:, k:k + 1], tny[:, 0:16], ALU.mult, ALU.subtract)
                    S.stt(h0s[:, 1, k, :], g3i, cN[:, k:k + 1], tny[:, 16:32], ALU.mult, ALU.add)

            def st3(k):
                c, q = divmod(k, 4)
                w, cb, sb_, sl, pwr, pwi = ctx[k]
                if k % 12 == 0:
                    stt_['C'] = piece(l, 'C%d' % (k // 12), slot=2).rearrange("p (k r s) -> p k r s", k=12, r=2)
                py = lbank[c % 2]
                ta, tb, tc, td = st3t
                S.tt('pool', seg3(ta), seg3(w[4]), cb, ALU.mult)
                S.tt('pool', seg3(tb), seg3(w[5]), sb_, ALU.mult)
                S.tt('dve', hb[sl][0][:, :nt], ta[:, :nt], tb[:, :nt], ALU.subtract)
                S.tt('pool', seg3(tc), seg3(w[5]), cb, ALU.mult)
                S.tt('pool', seg3(td), seg3(w[4]), sb_, ALU.mult)
                S.tt('pool', hb[sl][1][:, :nt], tc[:, :nt], td[:, :nt], ALU.add)
                Ck = stt_['C'][:, k % 12]
                S.mm(py[:, :nt], Ck[:, 0, :], hb[sl][0][:, :nt], start=(q == 0), stop=False)
                S.mm(py[:, :nt], Ck[:, 1, :], hb[sl][1][:, :nt], start=False, stop=(q == 3))
                if q == 3:
                    S.stt(ytmp[c % 2][:, :nt], uf[c % 2][:, :nt], s5d[:, l, 0, c:c + 1], py[:, :nt], ALU.mult, ALU.add)
                    S.act(zT[:, c, :nt], ytmp[c % 2][:, :nt], AF.Gelu_apprx_tanh)

            for it in range(24 + 2):
                if it < 24:
                    st1(it)
                if 0 <= it - 1 < 24:
                    st2(it - 1)
                if it - 2 >= 0:
                    st3(it - 2)
            Gp = None
            for c in range(6):
                if c % 3 == 0:
                    Gp = W3(piece(l, 'glu%d' % (c // 3)), 6)
                pg = bank()
                for kc in range(6):
                    S.mm(pg[:, :nt], Gp[:, kc, (c % 3) * 128:(c % 3 + 1) * 128], zT[:, kc, :nt], start=(kc == 0), stop=(kc == 5))
                S.act(gt[c % 2][:, :nt], pg[:, :nt], AF.Sigmoid, bias=s5d[:, l, 1, c:c + 1])
                S.tt('dve', yT[:, c, :nt], zT[:, c, :nt], gt[c % 2][:, :nt], ALU.mult)
            if kind == 'S':
                for r_, nm in enumerate(('s_s5_re', 's_s5_im')):
                    tr_out(lambda cc: h0s[:, r_, cc, :], big, 16, 24)
                    S.dma('pool', dr[nm][l], big[0:16, :])
            branch(l, yT, 6, 'wa', 'ga', nt, True)

        def s5_store_prompt(l):
            for r_, nm in enumerate(('p_s5_re', 'p_s5_im')):
                pb = bank()
                S.tr(pb[0:24, 0:128], st_s5[:, l, r_, :], ident[:])
                S.cp('act', big[0:24, r_ * 128:(r_ + 1) * 128], pb[0:24, 0:128])
                S.dma('pool', dr[nm][l], big[0:24, r_ * 128:(r_ + 1) * 128])

        def lru_phase(l, nt, kind):
            if kind == 'S':
                xb4 = xbe[:, :, 0:112].rearrange("p c (b j) -> p c b j", j=7)
                S.dma('sp', big[0:48, 0:768], dr['cache_lru_conv'][l])
                for c0 in range(0, 6, 4):
                    n = min(4, 6 - c0)
                    pb = bank()
                    for j in range(n):
                        S.tr(pb[:, j * 48:(j + 1) * 48], big[0:48, (c0 + j) * 128:(c0 + j + 1) * 128], ident[0:48, 0:48])
                    for j in range(n):
                        S.cp('act', xb4[:, c0 + j, :, 0:3], pb[:, j * 48:(j + 1) * 48].rearrange("p (b j) -> p b j", j=3))
                S.dma('sp', big[0:16, 1024:1792], dr['state_lru'][l])
                pb = bank()
                for j in range(6):
                    S.tr(pb[:, j * 16:(j + 1) * 16], big[0:16, 1024 + j * 128:1024 + (j + 1) * 128], ident[0:16, 0:16])
                S.cp('act', h0l[:, :, :], pb[:, 0:96].rearrange("p (j r) -> p j r", j=6))
            XB = GB = None
            LGp = None
            for c in range(6):
                if c % 3 == 0:
                    XB = W3(piece(l, 'xb%d' % (c // 3)), 8)
                    GB = W3(piece(l, 'gb%d' % (c // 3)), 8)
                if c == 0:
                    LGp = piece(l, 'LG').rearrange("p (g c s) -> p g c s", g=2, c=6)
                px = bank()
                pg = bank()
                for kc in range(8):
                    S.mm(px[:, :nt], XB[:, kc, (c % 3) * 128:(c % 3 + 1) * 128], hT[:, kc, :nt], start=(kc == 0), stop=(kc == 7))
                for kc in range(8):
                    S.mm(pg[:, :nt], GB[:, kc, (c % 3) * 128:(c % 3 + 1) * 128], hT[:, kc, :nt], start=(kc == 0), stop=(kc == 7))
                b0, b1, b2, b3 = lw[(c % 4) * 4:(c % 4) * 4 + 4]
                xcb = lwb[c % 4]
                cwv = [lruc[:, l, j, c:c + 1] for j in range(4)]
                cbv = lruc[:, l, 4, c:c + 1]
                if kind == 'P':
                    S.cp('act', xbe[:, c, 0:3], st_lc[:, l, c, :])
                    S.cp('act', xbe[:, c, 3:3 + nt], px[:, :nt])
                    S.cp('act', st_lc[:, l, c, :], xbe[:, c, nt:nt + 3])
                    xc = b0[:, :nt]
                    S.ts('dve', xc, xbe[:, c, 3:3 + nt], cwv[3], ALU.mult, cbv, ALU.add)
                    for j in (2, 1, 0):
                        S.stt(xc, xbe[:, c, j:j + nt], cwv[j], xc, ALU.mult, ALU.add)
                else:
                    xv = xbe[:, c, 0:112].rearrange("p (b j) -> p b j", j=7)
                    S.cp('act', xv[:, :, 3:7], px[:, :64].rearrange("p (b t) -> p b t", t=4))
                    xc = b0[:, :64]
                    xc3 = xc.rearrange("p (b t) -> p b t", t=4)
                    S.ts('dve', xc3, xv[:, :, 3:7], cwv[3], ALU.mult, cbv, ALU.add)
                    for j in (2, 1, 0):
                        S.stt(xc3, xv[:, :, j:j + 4], cwv[j], xc3, ALU.mult, ALU.add)
                S.cp('act', xcb[:, :nt], xc)
                pr = bank()
                pi_ = bank()
                S.mm(pr[:, :nt], LGp[:, 0, c, :], xcb[:, :nt], True, True)
                S.mm(pi_[:, :nt], LGp[:, 1, c, :], xcb[:, :nt], True, True)
                S.act(b1[:, :nt], pr[:, :nt], AF.Sigmoid, bias=lruc[:, l, 5, c:c + 1])
                S.act(b2[:, :nt], pi_[:, :nt], AF.Sigmoid, bias=lruc[:, l, 6, c:c + 1])
                S.act(b3[:, :nt], b1[:, :nt], AF.Exp, scale=lruc[:, l, 8, c:c + 1])
                S.act(b3[:, :nt], b3[:, :nt], AF.Sqrt, bias=1.0, scale=-1.0)
                S.act(b1[:, :nt], b1[:, :nt], AF.Exp, scale=lruc[:, l, 7, c:c + 1])
                S.tt('pool', b2[:, :nt], b2[:, :nt], xc, ALU.mult)
                S.tt('pool', b2[:, :nt], b2[:, :nt], b3[:, :nt], ALU.mult)
                if kind == 'P':
                    S.scan(b0[:, :nt], b1[:, :nt], b2[:, :nt], st_lru[:, l, c:c + 1])
                    S.cp('act', st_lru[:, l, c:c + 1], b0[:, nt - 1:nt])
                else:
                    a3 = b1[:, :64].rearrange("p (b t) -> p b t", t=4)
                    bv3 = b2[:, :64].rearrange("p (b t) -> p b t", t=4)
                    S.tt('dve', b3[:, 0:16], a3[:, :, 0], h0l[:, c, :], ALU.mult)
                    S.tt('dve', bv3[:, :, 0], bv3[:, :, 0], b3[:, 0:16], ALU.add)
                    S.tt('dve', b1[:, :64], b1[:, :64], cZM[:, :], ALU.mult)
                    S.scan(b0[:, :64], b1[:, :64], b2[:, :64], 0.0)
                    S.cp('act', h0l[:, c, :], b0[:, :64].rearrange("p (b t) -> p b t", t=4)[:, :, 3])
                S.act(b3[:, :nt], pg[:, :nt], AF.Gelu_apprx_tanh)
                S.tt('dve', yT[:, c, :nt], b0[:, :nt], b3[:, :nt], ALU.mult)
            if kind == 'S':
                tr_out(lambda cc: h0l[:, cc, :], big, 16, 6)
                S.dma('pool', dr['s_lru_h'][l], big[0:16, 0:768])
                for cc in range(6):
                    xv = xbe[:, cc, 0:112].rearrange("p (b j) -> p b j", j=7)
                    S.cp('act', lw[0][:, cc * 48:(cc + 1) * 48].rearrange("p (b j) -> p b j", j=3), xv[:, :, 4:7])
                tr_out(lambda cc: lw[0][:, cc * 48:(cc + 1) * 48], big[:, 1024:1792], 48, 6)
                S.dma('pool', dr['s_lru_conv'][l], big[0:48, 1024:1792])
            branch(l, yT, 6, 'wb', 'gtb', nt, False)

        def lru_store_prompt(l):
            pb = bank()
            S.tr(pb[0:6, 0:128], st_lru[:, l, :], ident[:])
            for j in range(3):
                S.tr(pb[0:6, (j + 1) * 128:(j + 2) * 128], st_lc[:, l, :, j], ident[:])
            S.cp('act', big[0:6, 512:1024], pb[0:6, 0:512])
            S.dma('pool', dr['p_lru_h'][l], big[0:6, 512:640])
            for j in range(3):
                S.dma('pool', dr['p_lru_conv'][l, j], big[0:6, 640 + j * 128:768 + j * 128])

        def ret_phase(l, nt, kind):
            S.dma('sp', normg[:, :], dr['ret_norm_g'][l].partition_broadcast(128))
            if kind == 'S':
                L, nb, var = 64, 1, 2
            else:
                L = min(nt, 128)
                nb = nt // L
                var = 0 if L == 128 else 1
            for (nm, dstT) in (('q', qT), ('k', kT)):
                Wq = W3(piece(l, nm), 8)
                for c in range(4):
                    pq = bank()
                    for kc in range(8):
                        S.mm(pq[:, :nt], Wq[:, kc, c * 128:(c + 1) * 128], hT[:, kc, :nt], start=(kc == 0), stop=(kc == 7))
                    S.cp('act', rw[0][:, :nt], pq[:, :nt])
                    S.tt('dve', rw[1][:, :nt], pq[:, :nt], ropeT[:, 0, :nt], ALU.mult)
                    ps_ = bank()
                    S.mm(ps_[:, :nt], perm[:], rw[0][:, :nt], True, True)
                    S.tt('dve', rw[2][:, :nt], ps_[:, :nt], ropeT[:, 1, :nt], ALU.mult)
                    S.tt('pool', rw[1][:, :nt], rw[1][:, :nt], rw[2][:, :nt], ALU.add)
                    S.cp('act', dstT[:, c, :nt], rw[1][:, :nt])
                    if nm == 'q':
                        if kind == 'S':
                            S.tt('dve', qg[:, c, :64], rw[1][:, :64], cGQS[:, c, :], ALU.mult)
                        else:
                            S.tt('dve', qg[:, c, :nt].rearrange("p (s t) -> p s t", s=nb), rw[1][:, :nt].rearrange("p (s t) -> p s t", s=nb),
                                 cGQ[:, c, 0:L].unsqueeze(1).to_broadcast([128, nb, L]), ALU.mult)
            Wv = [W3(piece(l, 'v0'), 8), W3(piece(l, 'v1'), 8)]
            for b in range(nb):
                for half in range(2):
                    pv = bank()
                    for kc in range(8):
                        S.mm(pv[0:L, :], hT[:, kc, b * L:(b + 1) * L], Wv[half][:, kc, :], start=(kc == 0), stop=(kc == 7))
                    S.cp('act', vtok[0:L, b, half * 512:(half + 1) * 512], pv[0:L, :])
            Wg = [W3(piece(l, 'gc0'), 8), W3(piece(l, 'gc1'), 8)]
            for b in range(nb):
                blk = slice(b * L, (b + 1) * L)
                for half in range(2):
                    pg = bank()
                    for kc in range(8):
                        S.mm(pg[0:L, :], hT[:, kc, blk], Wg[half][:, kc, :], start=(kc == 0), stop=(kc == 7))
                    S.act(gsn[0:L, half * 512:(half + 1) * 512], pg[0:L, :], AF.Silu)
                S.tt('pool', gsn[0:L, :], gsn[0:L, :], normg[0:L, :], ALU.mult)
                for c in range(4):
                    pk = bank()
                    pkb = pk[:].bitcast(BF16)
                    S.tr(pkb[0:L, 0:128], kT[:, c, blk], identb[:])
                    S.tt('dve', kdec[0:L, b, c * 128:(c + 1) * 128], pkb[0:L, 0:128], cDK[0:L, var, c * 128:(c + 1) * 128], ALU.mult)
                for hg in range(2):
                    psE = bank()
                    psO = bank()
                    W_ = 128 if kind == 'P' else 64
                    Lc = L
                    for j in range(4):
                        h = hg * 4 + j
                        c = h // 2
                        r0 = (h % 2) * 64
                        ps = psE if h % 2 == 0 else psO
                        S.mm(ps[0:L, (j // 2) * W_:(j // 2) * W_ + Lc], kT[r0:r0 + 64, c, blk], qT[r0:r0 + 64, c, blk], True, True)
                    DTt = cDT if kind == 'P' else cDTS
                    for e_, ps in ((0, psE), (1, psO)):
                        S.tt('dve', PT[hg][0:L, e_::2, 0:Lc], ps[0:L, 0:2 * W_].rearrange("p (j n) -> p j n", j=2)[:, :, 0:Lc],
                             DTt[0:L, hg * 4 + e_:hg * 4 + 4:2, 0:Lc], ALU.mult)
                    pso = lbank[0]
                    piE = lbank[1]
                    piO = lbank[2]
                    for j in range(4):
                        h = hg * 4 + j
                        c = h // 2
                        r0 = (h % 2) * 64
                        pi_b = piE if h % 2 == 0 else piO
                        pic = slice((j // 2) * 128, (j // 2 + 1) * 128)
                        S.mm(pso[0:L, j * 128:(j + 1) * 128], PT[hg][0:L, j, 0:Lc], vtok[0:L, b, h * 128:(h + 1) * 128], True, True)
                        if kind == 'P':
                            S.mm(pi_b[0:L, pic], qg[r0:r0 + 64, c, blk], st_retb[r0:r0 + 64, l, c, :], True, True)
                        else:
                            if h % 2 == 0:
                                S.tt('dve', qgm[:, :, :], qg[:, c, 0:64].unsqueeze(1).to_broadcast([128, 16, 64]), cMF[:, :, :], ALU.mult)
                            sr = srs[0]
                            for hb_ in range(2):
                                S.dma('sp', sr[r0:r0 + 64, :, :], dr['state_ret'][l, 8 * hb_:8 * hb_ + 8, h].rearrange("b d e -> d b e"))
                                S.cp('act', srb[r0:r0 + 64, :, :], sr[r0:r0 + 64, :, :])
                                for bb in range(8):
                                    S.mm(pi_b[0:64, pic], qgm[r0:r0 + 64, 8 * hb_ + bb, :], srb[r0:r0 + 64, bb, :], (hb_ == 0 and bb == 0), (hb_ == 1 and bb == 7))
                                S.tt('dve', vm[0:64, :, :], vtok[0:64, 0, h * 128:(h + 1) * 128].unsqueeze(1).to_broadcast([64, 8, 128]),
                                     cMT[0:64, 8 * hb_:8 * hb_ + 8].unsqueeze(2).to_broadcast([64, 8, 128]), ALU.mult)
                                for qq in range(2):
                                    pu_ = bank()
                                    S.mm(pu_[r0:r0 + 64, :], kdec[0:64, 0, h * 64:(h + 1) * 64], vm[0:64, 4 * qq:4 * qq + 4, :], True, True)
                                    S.stt(sr[r0:r0 + 64, 4 * qq:4 * qq + 4, :], sr[r0:r0 + 64, 4 * qq:4 * qq + 4, :], G4[h],
                                          pu_[r0:r0 + 64, :].rearrange("p (b e) -> p b e", b=4), ALU.mult, ALU.add)
                                S.dma('pool', dr['s_ret'][l, 8 * hb_:8 * hb_ + 8, h].rearrange("b d e -> d b e"), sr[r0:r0 + 64, :, :])
                    S.cp('act', otm[0:L, hg * 512:(hg + 1) * 512], pso[0:L, :])
                    o4 = otm[0:L, hg * 512:(hg + 1) * 512].rearrange("p (j e) -> p j e", j=4)
                    S.tt('dve', o4[:, 0::2, :], o4[:, 0::2, :], piE[0:L, 0:256].rearrange("p (j e) -> p j e", j=2), ALU.add)
                    S.tt('dve', o4[:, 1::2, :], o4[:, 1::2, :], piO[0:L, 0:256].rearrange("p (j e) -> p j e", j=2), ALU.add)
                    S.act(rw[0][0:L, :], otm[0:L, hg * 512:(hg + 1) * 512], AF.Square)
                    S.red(stat[0:L, 0, hg * 4:hg * 4 + 4], otm[0:L, hg * 512:(hg + 1) * 512].rearrange("p (j e) -> p j e", j=4))
                    S.red(stat[0:L, 1, hg * 4:hg * 4 + 4], rw[0][0:L, :].rearrange("p (j e) -> p j e", j=4))
                S.ts('dve', stat[0:L, 2, :], stat[0:L, 0, :], 1.0 / 128, ALU.mult)
                S.tt('dve', stat[0:L, 3, :], stat[0:L, 2, :], stat[0:L, 2, :], ALU.mult)
                S.stt(stat[0:L, 4, :], stat[0:L, 1, :], 1.0 / 128, stat[0:L, 3, :], ALU.mult, ALU.subtract)
                S.act(stat[0:L, 5, :], stat[0:L, 4, :], AF.Sqrt, bias=EPS)
                S.recip(stat[0:L, 5, :], stat[0:L, 5, :])
                o3 = otm[0:L, :].rearrange("p (h e) -> p h e", h=8)
                S.tt('dve', o3, o3, stat[0:L, 2, :].unsqueeze(2).to_broadcast([L, 8, 128]), ALU.subtract)
                S.tt('dve', o3, o3, stat[0:L, 5, :].unsqueeze(2).to_broadcast([L, 8, 128]), ALU.mult)
                S.tt('dve', yct[0:L, :], otm[0:L, :], gsn[0:L, :], ALU.mult)
                pt = bank()
                ptb = pt[:].bitcast(BF16)
                for h in range(8):
                    S.tr(ptb[:, h * L:(h + 1) * L], yct[0:L, h * 128:(h + 1) * 128], identb[0:L, 0:L])
                S.cp('act', yT[:, :, blk], ptb[:, 0:8 * L].rearrange("p (h n) -> p h n", h=8))
                if kind == 'P':
                    pu_ = bank()
                    for h in range(8):
                        c = h // 2
                        r0 = (h % 2) * 64
                        S.mm(pu_[r0:r0 + 64, c * 128:(c + 1) * 128], kdec[0:L, b, h * 64:(h + 1) * 64], vtok[0:L, b, h * 128:(h + 1) * 128], True, True)
                    S.tt('pool', st_ret[:, l, :, :], st_ret[:, l, :, :], cGC[:, var, :, :], ALU.mult)
                    S.tt('dve', st_ret[:, l, :, :], st_ret[:, l, :, :], pu_[:, :].rearrange("p (c e) -> p c e", c=4), ALU.add)
                    S.cp('act', st_retb[:, l, :, :], st_ret[:, l, :, :])
            branch(l, yT, 8, 'wc', 'gtc', nt, False)

        def ret_store_prompt(l):
            S.dma('pool', dr['p_ret'][l].rearrange("(c two) d e -> (two d) c e", two=2), st_ret[:, l, :, :])

        def outproj(l, nt):
            for c in range(8):
                S.cp('act', hT[:, c, :nt], mA[:, c, :nt])
            for half in range(2):
                Wo = W3(piece(l, 'wo%d' % half), 8)
                for oc4 in range(4):
                    oc = half * 4 + oc4
                    po = bank()
                    for kc in range(8):
                        S.mm(po[:, :nt], Wo[:, kc, oc4 * 128:(oc4 + 1) * 128], hT[:, kc, :nt], start=(kc == 0), stop=(kc == 7))
                    S.cp('act', mA[:, oc, :nt], po[:, :nt])
            resid_add(l, 1, mA, nt)

        def ffn_phase(l, nt, kind):
            norm_to_h(l, 2, nt)
            if kind == 'S':
                S.dma('sp', big[0:32, 0:DFF], dr['cache_ffn_conv'][l])
                for c0 in range(0, 22, 4):
                    n = min(4, 22 - c0)
                    pb = bank()
                    for j in range(n):
                        S.tr(pb[:, j * 32:(j + 1) * 32], big[0:32, (c0 + j) * 128:(c0 + j + 1) * 128], ident[0:32, 0:32])
                    S.cp('act', h0f[:, c0:c0 + n, :], pb[:, 0:n * 32].rearrange("p (j r) -> p j r", j=n))
            for jj in range(11):
                Up = W3(piece(l, 'up%d' % jj), 8)
                for s_ in range(2):
                    ch = 2 * jj + s_
                    pgf = bank()
                    puf = bank()
                    for kc in range(8):
                        S.mm(pgf[:, :nt], Up[:, kc, s_ * 128:(s_ + 1) * 128], hT[:, kc, :nt], start=(kc == 0), stop=(kc == 7))
                    for kc in range(8):
                        S.mm(puf[:, :nt], Up[:, kc, 256 + s_ * 128:256 + (s_ + 1) * 128], hT[:, kc, :nt], start=(kc == 0), stop=(kc == 7))
                    ge = gfe[ch % 4]
                    acc = fw[ch % 4]
                    w0, w1, w2, wb_ = [ffnc[:, l, i, ch:ch + 1] for i in range(4)]
                    if kind == 'P':
                        S.cp('act', ge[:, 0:2], st_ff[:, l, ch, :])
                        S.cp('act', ge[:, 2:2 + nt], pgf[:, :nt])
                        S.cp('act', st_ff[:, l, ch, :], ge[:, nt:nt + 2])
                        a_ = acc[:, :nt]
                        S.ts('dve', a_, ge[:, 2:2 + nt], w2, ALU.mult, wb_, ALU.add)
                        S.stt(a_, ge[:, 1:1 + nt], w1, a_, ALU.mult, ALU.add)
                        S.stt(a_, ge[:, 0:nt], w0, a_, ALU.mult, ALU.add)
                    else:
                        g3 = ge[:, 0:96].rearrange("p (b j) -> p b j", j=6)
                        S.cp('act', g3[:, :, 0:2], h0f[:, ch, :].rearrange("p (b j) -> p b j", j=2))
                        S.cp('act', g3[:, :, 2:6], pgf[:, :64].rearrange("p (b t) -> p b t", t=4))
                        S.cp('act', h0f[:, ch, :].rearrange("p (b j) -> p b j", j=2), g3[:, :, 4:6])
                        a_ = acc[:, :64]
                        a3_ = a_.rearrange("p (b t) -> p b t", t=4)
                        S.ts('dve', a3_, g3[:, :, 2:6], w2, ALU.mult, wb_, ALU.add)
                        S.stt(a3_, g3[:, :, 1:5], w1, a3_, ALU.mult, ALU.add)
                        S.stt(a3_, g3[:, :, 0:4], w0, a3_, ALU.mult, ALU.add)
                    gl = fw[4 + ch % 4]
                    S.act(gl[:, :nt], a_, AF.Gelu_apprx_tanh)
                    S.tt('dve', aT[:, ch, :nt], gl[:, :nt], puf[:, :nt], ALU.mult)
            for oc in range(8):
                Dn = W3(piece(l, 'dn%d' % oc), 22)
                pf = bank()
                for kc in range(22):
                    S.mm(pf[:, :nt], Dn[:, kc, :], aT[:, kc, :nt], start=(kc == 0), stop=(kc == 21))
                S.cp('act', mA[:, oc, :nt], pf[:, :nt])
            if kind == 'S':
                tr_out(lambda cc: h0f[:, cc, :], big, 32, 22)
                S.dma('pool', dr['s_ffn_conv'][l], big[0:32, 0:DFF])
            resid_add(l, 3, mA, nt)

        def ffn_store_prompt(l):
            pb = bank()
            for j in range(2):
                S.tr(pb[0:22, j * 128:(j + 1) * 128], st_ff[:, l, :, j], ident[:])
            S.cp('act', big[0:22, 1024:1280], pb[0:22, 0:256])
            for j in range(2):
                S.dma('pool', dr['p_ffn_conv'][l, j], big[0:22, 1024 + j * 128:1152 + j * 128])

        S.tag = 'main_pre'
        tiles = [('P', 512, 0), ('P', 512, 512), ('P', 512, 1024), ('P', 512, 1536), ('P', 16, 2048), ('S', 64, 2064)]
        if debug and debug.startswith('T'):
            tiles = [tiles[int(ch_)] for ch_ in debug[1:].split('cut')[0]]
        nlayers = DEPTH
        xi = [0]
        for (kind, nt, pos0) in tiles:
            if kind == 'P':
                L = min(nt, 128)
                for b in range(nt // L):
                    xt_ = xtok[xi[0] % 2]
                    xi[0] += 1
                    t0 = pos0 + b * L
                    if t0 == 0:
                        S.dma('sp', xt_[0:16, :], dr['meta_tokens'])
                        S.dma('sp', xt_[16:128, :], dr['x_prompt'][0:112, :])
                    else:
                        S.dma('sp', xt_[0:L, :], dr['x_prompt'][t0 - 16:t0 - 16 + L, :])
                    tr_in(lambda c0, n, b=b, L=L: xT[:, c0:c0 + n, b * L:(b + 1) * L], xt_, L, 8)
            else:
                xt_ = xtok[xi[0] % 2]
                xi[0] += 1
                S.dma('sp', xt_[0:64, :], dr['x_sample'])
                tr_in(lambda c0, n: xT[:, c0:c0 + n, 0:64], xt_, 64, 8)
            S.dma('sp', ropeT[:, 0, :nt], dr['c_rope_c'][:, pos0:pos0 + nt])
            S.dma('sp', ropeT[:, 1, :nt], dr['c_rope_s'][:, pos0:pos0 + nt])
            for l in range(nlayers):
                tg = '%s%d_L%d_' % (kind, pos0, l)
                S.tag = tg + 'a_s5'
                bmod[0] = 5
                norm_to_h(l, 0, nt)
                s5_phase(l, nt, kind)
                S.tag = tg + 'b_lru'
                bmod[0] = 8
                lru_phase(l, nt, kind)
                S.tag = tg + 'c_ret'
                bmod[0] = 5
                ret_phase(l, nt, kind)
                S.tag = tg + 'd_out'
                bmod[0] = 8
                outproj(l, nt)
                S.tag = tg + 'e_ffn'
                ffn_phase(l, nt, kind)
                S.tag = tg + 'f_post'
                if kind == 'P' and pos0 == 2048:
                    s5_store_prompt(l)
                    lru_store_prompt(l)
                    ret_store_prompt(l)
                    ffn_store_prompt(l)
            S.frozen = False
            if kind == 'P':
                L = min(nt, 128)
                for b in range(nt // L):
                    xt_ = xtok[xi[0] % 2]
                    xi[0] += 1
                    t0 = pos0 + b * L
                    tr_out(lambda cc, b=b, L=L: xT[:, cc, b * L:(b + 1) * L], xt_, L, 8)
                    if t0 == 0:
                        S.dma('pool', dr['y_prompt'][0:112, :], xt_[16:128, :])
                    else:
                        S.dma('pool', dr['y_prompt'][t0 - 16:t0 - 16 + L, :], xt_[0:L, :])
            else:
                xt_ = xtok[xi[0] % 2]
                xi[0] += 1
                tr_out(lambda cc: xT[:, cc, 0:64], xt_, 64, 8)
                S.dma('pool', dr['y_sample'], xt_[0:64, :])
        S.frozen = False
        if debug:
            S.dma('sp', dr['dbg'][:, 0:DEPTH * 8 * 24], s5c[:].rearrange("p l a k -> p (l a k)"))
            S.dma('sp', dr['dbg'][:, 1000:1000 + 120], lruc[:].rearrange("p l a k -> p (l a k)"))
        ncd.__exit__(None, None, None)
        global LAST_SCHED
        LAST_SCHED = S
        with nc.Block() as block:
            S.emit(block, sems, dsems)
    return nc


def make_in_maps(inputs):
    consts = host_consts()
    maps = []
    for c in range(8):
        m = {}
        m['x_prompt'] = np.ascontiguousarray(inputs['x_prompt'][c])
        m['x_sample'] = np.ascontiguousarray(inputs['x_sample'][c * 16:(c + 1) * 16]).reshape(64, D)
        m['state_s5_re'] = np.ascontiguousarray(inputs['state_s5_re'][:, c * 16:(c + 1) * 16]).reshape(DEPTH, 16, 3072)
        m['state_s5_im'] = np.ascontiguousarray(inputs['state_s5_im'][:, c * 16:(c + 1) * 16]).reshape(DEPTH, 16, 3072)
        m['state_lru'] = np.ascontiguousarray(inputs['state_lru'][:, c * 16:(c + 1) * 16])
        m['cache_lru_conv'] = np.ascontiguousarray(inputs['cache_lru_conv'][:, c * 16:(c + 1) * 16]).reshape(DEPTH, 48, 768)
        m['state_ret'] = np.ascontiguousarray(inputs['state_ret'][:, c * 16:(c + 1) * 16])
        m['cache_ffn_conv'] = np.ascontiguousarray(inputs['cache_ffn_conv'][:, c * 16:(c + 1) * 16]).reshape(DEPTH, 32, DFF)
        m['meta_tokens'] = np.ascontiguousarray(inputs['meta_tokens'])
        for n_ in WNAMES:
            m[n_] = np.ascontiguousarray(inputs[n_])
        for k_, v_ in consts.items():
            m['c_' + k_] = v_
        maps.append(m)
    return maps


_NC_CACHE = {}


def kernel(**inputs):
    inputs = {k: np.asarray(v) for k, v in inputs.items()}
    if 'nc' not in _NC_CACHE:
        _NC_CACHE['nc'] = build()
    nc = _NC_CACHE['nc']
    res = run_bass_kernel_spmd(nc, make_in_maps(inputs), core_ids=list(range(8)))
    R = res.results
    f = np.float32

    def cat(name, shape_per):
        return [np.asarray(R[c][name]) for c in range(8)]
    y_prompt = np.stack([R[c]['y_prompt'] for c in range(8)]).astype(f)
    y_sample = np.concatenate([np.asarray(R[c]['y_sample']).reshape(16, 4, D) for c in range(8)], 0).astype(f)
    p_s5_re = np.stack([np.asarray(R[c]['p_s5_re']).reshape(DEPTH, 48, 64) for c in range(8)], 1).astype(f)
    p_s5_im = np.stack([np.asarray(R[c]['p_s5_im']).reshape(DEPTH, 48, 64) for c in range(8)], 1).astype(f)
    p_lru_h = np.stack([np.asarray(R[c]['p_lru_h']).reshape(DEPTH, 768) for c in range(8)], 1).astype(f)
    p_lru_conv = np.stack([np.asarray(R[c]['p_lru_conv']).reshape(DEPTH, 3, 768) for c in range(8)], 1).astype(f)
    p_ret = np.stack([np.asarray(R[c]['p_ret']) for c in range(8)], 1).astype(f)
    p_ffn = np.stack([np.asarray(R[c]['p_ffn_conv']).reshape(DEPTH, 2, DFF) for c in range(8)], 1).astype(f)
    s_s5_re = np.concatenate([np.asarray(R[c]['s_s5_re']).reshape(DEPTH, 16, 48, 64) for c in range(8)], 1).astype(f)
    s_s5_im = np.concatenate([np.asarray(R[c]['s_s5_im']).reshape(DEPTH, 16, 48, 64) for c in range(8)], 1).astype(f)
    s_lru_h = np.concatenate([np.asarray(R[c]['s_lru_h']) for c in range(8)], 1).astype(f)
    s_lru_conv = np.concatenate([np.asarray(R[c]['s_lru_conv']).reshape(DEPTH, 16, 3, 768) for c in range(8)], 1).astype(f)
    s_ret = np.concatenate([np.asarray(R[c]['s_ret']) for c in range(8)], 1).astype(f)
    s_ffn = np.concatenate([np.asarray(R[c]['s_ffn_conv']).reshape(DEPTH, 16, 2, DFF) for c in range(8)], 1).astype(f)
    return (y_prompt, y_sample, p_s5_re, p_s5_im, p_lru_h, p_lru_conv, p_ret, p_ffn,
            s_s5_re, s_s5_im, s_lru_h, s_lru_conv, s_ret, s_ffn)
```
